# Optimizing a Trainium2 kernel written in Bass

```python
import jax, jax.numpy as jnp
from jax import lax
import numpy as np

D_MODEL = 1024
BATCH = 32
SEQ = 2048
DEPTH = 4

N_EVEN = (DEPTH + 1) // 2
N_ODD = DEPTH // 2

ATT_HEADS = 8
ATT_KV_HEADS = 2
ATT_HEAD_DIM = 64
WINDOW = 128
ROPE_THETA = 10000.0
ATT_WIDTH = ATT_HEADS * ATT_HEAD_DIM
KV_WIDTH = ATT_KV_HEADS * ATT_HEAD_DIM
POOL_WINDOWS = (2, 4, 8, 16)
POOL_GROUP = 128
POOL_WIDTH = POOL_GROUP * len(POOL_WINDOWS)
AB_IN = ATT_WIDTH + 2 * KV_WIDTH + POOL_WIDTH
AB_OUT = ATT_WIDTH + POOL_WIDTH
HGRN_EXPAND = 128
HGRN_HEADS = D_MODEL // HGRN_EXPAND
HGRN_DK = HGRN_EXPAND
HGRN_DV = D_MODEL // HGRN_HEADS
HGRN_KWIDTH = HGRN_HEADS * HGRN_DK
HGRN_VWIDTH = HGRN_HEADS * HGRN_DV
HGRN_IN = 2 * HGRN_KWIDTH + 2 * HGRN_VWIDTH
HGRN_CHUNK = 32
D_FF = 2816
NORM_EPS = 1e-6
GATE_EPS = 1e-6

kernel_name = "hybrid_swa_pool_hgrn2_macaron"


def rms_norm(x, gain):
    xf = x.astype(jnp.float32)
    y = xf * lax.rsqrt(jnp.mean(xf * xf, axis=-1, keepdims=True) + NORM_EPS)
    return (y * gain.astype(jnp.float32)).astype(x.dtype)


def swiglu(x, w_gate, w_up, w_down):
    return (jax.nn.silu(x @ w_gate) * (x @ w_up)) @ w_down


def rope(x, positions):
    half = x.shape[-1] // 2
    inv_freq = ROPE_THETA ** (-jnp.arange(half, dtype=jnp.float32) / half)
    ang = positions.astype(jnp.float32)[:, None] * inv_freq[None, :]
    cos = jnp.cos(ang)[None, :, None, :]
    sin = jnp.sin(ang)[None, :, None, :]
    xf = x.astype(jnp.float32)
    x1, x2 = xf[..., :half], xf[..., half:]
    return jnp.concatenate([x1 * cos - x2 * sin, x2 * cos + x1 * sin], axis=-1).astype(x.dtype)


def sliding_window_attention(q, k, v, sinks):
    B, T, H, d = q.shape
    nb = T // WINDOW
    G = H // ATT_KV_HEADS
    qb = q.reshape(B, nb, WINDOW, ATT_KV_HEADS, G, d).transpose(1, 0, 2, 3, 4, 5)

    def band_keys(a):
        ab = a.reshape(B, nb, WINDOW, ATT_KV_HEADS, d)
        prev = jnp.pad(ab, ((0, 0), (1, 0), (0, 0), (0, 0), (0, 0)))[:, :nb]
        return jnp.concatenate([prev, ab], axis=2).transpose(1, 0, 2, 3, 4)

    k_win, v_win = band_keys(k), band_keys(v)
    r = jnp.arange(WINDOW)[:, None]
    c = jnp.arange(2 * WINDOW)[None, :]
    rel = WINDOW + r - c
    band = (rel >= 0) & (rel < WINDOW)
    sink = sinks.astype(jnp.float32).reshape(ATT_KV_HEADS, G)[None, :, :, None, None]
    scale = d ** -0.5

    def block(args):
        qi, ki, vi, idx = args
        s = jnp.einsum('bqkgd,bskd->bkgqs', qi.astype(jnp.float32), ki.astype(jnp.float32)) * scale
        mask = band & ((c >= WINDOW) | (idx > 0))
        s = jnp.where(mask, s, -jnp.inf)
        m = jnp.maximum(jnp.max(s, axis=-1, keepdims=True), sink)
        p = jnp.where(mask, jnp.exp(s - m), 0.0)
        denom = jnp.sum(p, axis=-1, keepdims=True) + jnp.exp(sink - m)
        o = jnp.einsum('bkgqs,bskd->bqkgd', p / denom, vi.astype(jnp.float32))
        return o.astype(qi.dtype)

    out = lax.map(block, (qb, k_win, v_win, jnp.arange(nb)))
    return out.transpose(1, 0, 2, 3, 4, 5).reshape(B, T, H * d)


def multiscale_pool(u, w_pool, pool_scale):
    T = u.shape[1]
    uf = u.astype(jnp.float32)
    cs = jnp.cumsum(uf, axis=1)
    count = jnp.arange(1, T + 1, dtype=jnp.float32)[None, :, None]
    outs = []
    for gi, w in enumerate(POOL_WINDOWS):
        sl = slice(gi * POOL_GROUP, (gi + 1) * POOL_GROUP)
        cg = cs[..., sl]
        lag = jnp.pad(cg, ((0, 0), (w, 0), (0, 0)))[:, :T]
        mean = (cg - lag) / jnp.minimum(count, float(w))
        outs.append((mean - uf[..., sl]).astype(u.dtype) @ w_pool[gi])
    return jnp.concatenate(outs, axis=-1) * pool_scale


def hgrn2_chunkwise(q, f_logit, v, lb):
    B, T, H, dk = q.shape
    dv = v.shape[-1]
    C = HGRN_CHUNK
    nc = T // C
    lbf = lb.astype(jnp.float32).reshape(H, dk)
    z = f_logit.astype(jnp.float32)
    sig = jax.nn.sigmoid(z)
    f = lbf + (1.0 - lbf) * sig
    log_f = jnp.log(jnp.maximum(f, GATE_EPS))
    key = 1.0 - f
    qf = jax.nn.silu(q.astype(jnp.float32))

    def to_chunks(a):
        return a.reshape(B, nc, C, H, a.shape[-1]).swapaxes(0, 1)

    causal = jnp.tril(jnp.ones((C, C), dtype=bool))[None, :, :, None, None]

    def step(S, inp):
        qc, kc, gc, vc = inp
        Gc = jnp.cumsum(gc, axis=1)
        o_inter = jnp.einsum('bchk,bhkv->bchv', qc * jnp.exp(Gc), S)
        diff = jnp.where(causal, Gc[:, :, None] - Gc[:, None, :], 0.0)
        decay = jnp.where(causal, jnp.exp(diff), 0.0)
        scores = jnp.einsum('bthk,btshk,bshk->bhts', qc, decay, kc)
        o_intra = jnp.einsum('bhts,bshv->bthv', scores, vc)
        G_last = Gc[:, -1]
        S = jnp.exp(G_last)[..., None] * S + jnp.einsum(
            'bshk,bshv->bhkv', kc * jnp.exp(G_last[:, None] - Gc), vc)
        return S, o_inter + o_intra

    S0 = jnp.zeros((B, H, dk, dv), jnp.float32)
    _, o = lax.scan(step, S0, (to_chunks(qf), to_chunks(key), to_chunks(log_f),
                               to_chunks(v.astype(jnp.float32))))
    return o.swapaxes(0, 1).reshape(B, T, H, dv)


def attn_pool_mixer(h, positions, w_in, w_out, q_gain, k_gain, sinks, w_pool, pool_scale):
    B, T, _ = h.shape
    proj = h @ w_in
    q, k, v, u = jnp.split(proj, [ATT_WIDTH, ATT_WIDTH + KV_WIDTH, ATT_WIDTH + 2 * KV_WIDTH], axis=-1)
    q = rope(rms_norm(q.reshape(B, T, ATT_HEADS, ATT_HEAD_DIM), q_gain), positions)
    k = rope(rms_norm(k.reshape(B, T, ATT_KV_HEADS, ATT_HEAD_DIM), k_gain), positions)
    v = v.reshape(B, T, ATT_KV_HEADS, ATT_HEAD_DIM)
    a = sliding_window_attention(q, k, v, sinks)
    p = multiscale_pool(u, w_pool, pool_scale)
    return jnp.concatenate([a, p], axis=-1) @ w_out


def hgrn_mixer(h, w_in, w_out, out_gain, lb):
    B, T, _ = h.shape
    proj = h @ w_in
    q, f, i, g = jnp.split(proj, [HGRN_KWIDTH, 2 * HGRN_KWIDTH, 2 * HGRN_KWIDTH + HGRN_VWIDTH], axis=-1)
    o = hgrn2_chunkwise(q.reshape(B, T, HGRN_HEADS, HGRN_DK), f.reshape(B, T, HGRN_HEADS, HGRN_DK),
                        i.reshape(B, T, HGRN_HEADS, HGRN_DV), lb)
    gate = jax.nn.sigmoid(g.reshape(B, T, HGRN_HEADS, HGRN_DV).astype(jnp.float32))
    o = rms_norm(o * gate, out_gain).astype(h.dtype).reshape(B, T, HGRN_VWIDTH)
    return o @ w_out


def setup_inputs(seed: int = 0) -> dict:
    key = jax.random.key(seed)
    ks = jax.random.split(key, 16)
    f32 = jnp.float32

    def w(k, shape, fan_in):
        return jax.random.normal(k, shape, f32) * fan_in ** -0.5

    x = jax.random.normal(ks[0], (BATCH, SEQ, D_MODEL), f32)
    positions = jnp.arange(SEQ, dtype=jnp.int32)
    norm_gains = 1.0 + 0.1 * jax.random.normal(ks[1], (DEPTH, 3, D_MODEL), f32)
    ffn_w_gate = w(ks[2], (DEPTH, 2, D_MODEL, D_FF), D_MODEL)
    ffn_w_up = w(ks[3], (DEPTH, 2, D_MODEL, D_FF), D_MODEL)
    ffn_w_down = w(ks[4], (DEPTH, 2, D_FF, D_MODEL), D_FF)
    ab_w_in = w(ks[5], (N_EVEN, D_MODEL, AB_IN), D_MODEL)
    ab_w_out = w(ks[6], (N_EVEN, AB_OUT, D_MODEL), AB_OUT)
    q_norm_gain = 1.0 + 0.1 * jax.random.normal(ks[7], (N_EVEN, ATT_HEAD_DIM), f32)
    k_norm_gain = 1.0 + 0.1 * jax.random.normal(ks[8], (N_EVEN, ATT_HEAD_DIM), f32)
    attn_sinks = 0.5 * jax.random.normal(ks[9], (N_EVEN, ATT_HEADS), f32)
    pool_w = w(ks[10], (N_EVEN, len(POOL_WINDOWS), POOL_GROUP, POOL_GROUP), POOL_GROUP)
    pool_scale = 1.0 + 0.1 * jax.random.normal(ks[11], (N_EVEN, POOL_WIDTH), f32)
    c_w_in = w(ks[12], (N_ODD, D_MODEL, HGRN_IN), D_MODEL)
    c_w_out = w(ks[13], (N_ODD, HGRN_VWIDTH, D_MODEL), HGRN_VWIDTH)
    c_out_norm_gain = 1.0 + 0.1 * jax.random.normal(ks[14], (N_ODD, HGRN_DV), f32)
    lb_logits = jax.random.normal(ks[15], (N_ODD, HGRN_KWIDTH), f32)
    return {"x": x, "positions": positions, "norm_gains": norm_gains,
            "ffn_w_gate": ffn_w_gate, "ffn_w_up": ffn_w_up, "ffn_w_down": ffn_w_down,
            "ab_w_in": ab_w_in, "ab_w_out": ab_w_out, "q_norm_gain": q_norm_gain,
            "k_norm_gain": k_norm_gain, "attn_sinks": attn_sinks, "pool_w": pool_w,
            "pool_scale": pool_scale, "c_w_in": c_w_in, "c_w_out": c_w_out,
            "c_out_norm_gain": c_out_norm_gain, "lb_logits": lb_logits}


def reference(x, positions, norm_gains, ffn_w_gate, ffn_w_up, ffn_w_down,
              ab_w_in, ab_w_out, q_norm_gain, k_norm_gain, attn_sinks, pool_w,
              pool_scale, c_w_in, c_w_out, c_out_norm_gain, lb_logits):
    P = jax.nn.softmax(lb_logits.astype(jnp.float32), axis=0)
    lower_bounds = jnp.cumsum(P, axis=0) - P[0]
    for layer in range(DEPTH):
        h = rms_norm(x, norm_gains[layer, 0])
        x = x + 0.5 * swiglu(h, ffn_w_gate[layer, 0], ffn_w_up[layer, 0], ffn_w_down[layer, 0])
        h = rms_norm(x, norm_gains[layer, 1])
        j = layer // 2
        if layer % 2 == 0:
            x = x + attn_pool_mixer(h, positions, ab_w_in[j], ab_w_out[j], q_norm_gain[j],
                                    k_norm_gain[j], attn_sinks[j], pool_w[j], pool_scale[j])
        else:
            x = x + hgrn_mixer(h, c_w_in[j], c_w_out[j], c_out_norm_gain[j], lower_bounds[j])
        h = rms_norm(x, norm_gains[layer, 2])
        x = x + 0.5 * swiglu(h, ffn_w_gate[layer, 1], ffn_w_up[layer, 1], ffn_w_down[layer, 1])
    return x
```

```python
import math
import os
import numpy as np
from contextlib import ExitStack
import concourse.bass as bass
import concourse.mybir as mybir
from concourse.bass_utils import run_bass_kernel_spmd

F32 = mybir.dt.float32
BF16 = mybir.dt.bfloat16
I32 = mybir.dt.int32
AF = mybir.ActivationFunctionType
ALU = mybir.AluOpType

NCORES = 8
DEPTH = 4
D = 1024
T = 2048
DFF = 2816
NJ = 22
KC = 8
CB = 512
NCB = 4
EPS = 1e-6

C_NG = 0
C_QG = 96
C_PS = 104
C_OG = 112
C_LB = 114
C_SK = 130
C_IF = 138
C_SG = 139
NS = 140
F_ID = 0
F_HM = 128
F_RM = 256
NF = 768
B_ID = 0
B_OD = 128
B_BD = 256
B_O1 = 384
B_AB = 512
B_MP = 704
B_MO = 832
B_Q1 = 960
NBC = 1088


class Eng:
    def __init__(self, h, sem):
        self.h = h
        self.sem = sem
        self.count = 0
        self.waited = {}


class Slot:
    def __init__(self, sem):
        self.sem = sem
        self.count = 0


class Trk:
    def __init__(self):
        self.lw = {}
        self.rd = {}

    def _deps(self, eng, reads, writes):
        deps = {}

        def add(p, v):
            if deps.get(p, 0) < v:
                deps[p] = v

        for r in reads:
            t = self.lw.get(r)
            if t is not None:
                add(*t)
            if isinstance(r, tuple) and r[0] == "ps":
                for p, v in self.rd.get(r, {}).items():
                    if p is not eng:
                        add(p, v)
        for w in writes:
            t = self.lw.get(w)
            if t is not None:
                add(*t)
            for p, v in self.rd.get(w, {}).items():
                add(p, v)
        return deps

    def _wait(self, eng, deps):
        for p, v in deps.items():
            if eng.waited.get(p, 0) < v:
                eng.h.wait_ge(p.sem, v)
                eng.waited[p] = v

    def _commit(self, tok, reads, writes):
        p, v = tok
        for r in reads:
            d = self.rd.setdefault(r, {})
            if d.get(p, 0) < v:
                d[p] = v
        for w in writes:
            self.lw[w] = tok
            self.rd[w] = {}

    def op(self, eng, fn, reads=(), writes=()):
        self._wait(eng, self._deps(eng, reads, writes))
        ins = fn()
        eng.count += 1
        ins.then_inc(eng.sem, 1)
        self._commit((eng, eng.count), reads, writes)

    def dma(self, q, slot, out, in_, reads=(), writes=()):
        self._wait(q, self._deps(q, reads, writes))
        q.h.dma_start(out=out, in_=in_).then_inc(slot.sem, 16)
        slot.count += 16
        self._commit((slot, slot.count), reads, writes)


class Prog:
    def __init__(self, n_seq, parts=("ffn", "ab", "c"), depth=DEPTH):
        self.n_seq = n_seq
        self.parts = parts
        self.depth = depth
        self.nc = bass.Bass("TRN2", target_bir_lowering=False)
        self.T = Trk()
        self.slots = []
        self.build()

    def sem(self, name):
        return self.es.enter_context(self.nc.semaphore(name))

    def slot(self, name):
        s = Slot(self.sem(name))
        self.slots.append(s)
        return s

    def sb(self, es, name, shape, dt):
        self.uid = getattr(self, "uid", 0) + 1
        return es.enter_context(self.nc.sbuf_tensor(f"{name}_{self.uid}", shape, dt))

    def barrier(self):
        prods = self.engs + self.slots
        for e in self.engs:
            for p in prods:
                if p is e or p.count == 0:
                    continue
                if e.waited.get(p, 0) < p.count:
                    e.h.wait_ge(p.sem, p.count)
                    e.waited[p] = p.count

    def mmg(self, reads, writes, mms):
        pe = self.pe

        def fn():
            ins = None
            for (o, l, r, st, sp, kw) in mms:
                ins = pe.h.matmul(o, l, r, start=st, stop=sp, **kw)
            return ins

        self.T.op(pe, fn, reads, writes)

    def A(self, out, in_, func, reads, writes, **kw):
        act = self.act
        self.T.op(act, lambda: act.h.activation(out=out, in_=in_, func=func, **kw), reads, writes)

    def V(self, fn, reads, writes):
        self.T.op(self.dve, fn, reads, writes)

    def build(self):
        nc = self.nc
        ns = self.n_seq
        dr = lambda name, shape, dt=F32, kind="ExternalInput": nc.dram_tensor(name, shape, dt, kind=kind).ap()
        self.d_x = dr("x", [ns, KC, 128, T])
        self.d_y = dr("y", [ns, KC, 128, T], kind="ExternalOutput")
        self.d_pos = dr("pos", [1, T], I32)
        self.d_smalls = dr("smalls", [128, NS])
        self.d_cf32 = dr("cf32", [128, NF])
        self.d_cbf = dr("cbf", [128, NBC])
        self.d_wgu = dr("wgu", [DEPTH * 2, NJ, 128, 2 * KC * 128])
        self.d_wd = dr("wd", [DEPTH * 2, KC, 128, NJ * 128])
        self.d_wabin = dr("wabin", [2, 9, 128, 2 * KC * 128])
        self.d_wabout = dr("wabout", [2, 128, KC * D])
        self.d_wpool = dr("wpool", [2, 128, 4 * 128])
        self.d_wcin = dr("wcin", [2, 8, 128, 4 * KC * 128])
        self.d_wcout = dr("wcout", [2, 128, KC * D])

        with ExitStack() as es:
            self.es = es
            self.pe = Eng(nc.tensor, self.sem("s_pe"))
            self.act = Eng(nc.scalar, self.sem("s_act"))
            self.dve = Eng(nc.vector, self.sem("s_dve"))
            self.pool = Eng(nc.gpsimd, self.sem("s_pool"))
            self.sp = Eng(nc.sync, self.sem("s_sp"))
            self.engs = [self.pe, self.act, self.dve, self.pool, self.sp]
            self.ps = [es.enter_context(nc.psum_tensor(f"ps{i}", [128, CB], F32)) for i in range(8)]
            self.xT = self.sb(es, "xT", [128, KC, T], F32)
            self.smalls = self.sb(es, "smalls_sb", [128, NS], F32)
            self.cf = self.sb(es, "cf_sb", [128, NF], F32)
            self.cb = self.sb(es, "cb_sb", [128, NBC], BF16)
            self.cosT = self.sb(es, "cosT", [128, T], F32)
            self.sinT = self.sb(es, "sinT", [128, T], F32)
            self.esink = self.sb(es, "esink", [128, 8], F32)
            self.lb = self.sb(es, "lb", [128, 16], F32)
            self.oml = self.sb(es, "oml", [128, 16], F32)
            self.sl_gu = [self.slot(f"sl_gu{i}") for i in range(3)]
            self.sl_wd = [self.slot(f"sl_wd{i}") for i in range(3)]
            self.sl_xin = [self.slot(f"sl_xin{i}") for i in range(KC)]
            self.sl_xout = [self.slot(f"sl_xout{i}") for i in range(KC)]
            self.sl_misc = [self.slot(f"sl_misc{i}") for i in range(4)]
            self.sl_ab = [self.slot(f"sl_ab{i}") for i in range(3)]
            self.sl_wo = self.slot("sl_wo")
            self.sl_wp = self.slot("sl_wp")
            self.sl_hw = [self.slot(f"sl_hw{i}") for i in range(2)]
            self.rr = 0

            self.init_consts()
            self.load_x(0)
            for s in range(ns):
                s_next = s + 1 if s + 1 < ns else None
                for l in range(self.depth):
                    mix = (l % 2 == 0 and "ab" in self.parts) or (l % 2 == 1 and "c" in self.parts)
                    if "ffn" in self.parts and l == 0:
                        self.ffn([(0, 0)])
                    if mix:
                        if l % 2 == 0:
                            self.ab_mixer(l)
                        else:
                            self.c_mixer(l)
                    if "ffn" in self.parts:
                        if l + 1 < self.depth:
                            self.ffn([(l, 1), (l + 1, 0)])
                        else:
                            self.ffn([(l, 1)], xio=(s, s_next))
                if "ffn" not in self.parts:
                    self.store_x(s)
                    if s_next is not None:
                        self.load_x(s_next)
            sp = self.sp
            for sl in self.sl_xout:
                if sl.count > 0:
                    sp.h.wait_ge(sl.sem, sl.count)

    def init_consts(self):
        nc, Tk = self.nc, self.T
        sp, pool, dve, act = self.sp, self.pool, self.dve, self.act
        Tk.dma(sp, self.sl_misc[0], self.smalls[:], self.d_smalls, writes=["smalls"])
        Tk.dma(sp, self.sl_misc[1], self.cf[:], self.d_cf32, writes=["cf"])
        Tk.dma(pool, self.sl_misc[2], self.cb[:], self.d_cbf, writes=["cb"])
        with ExitStack() as ph:
            posi = self.sb(ph, "posi", [128, T], I32)
            ang = self.sb(ph, "ang", [128, T], F32)
            a2 = self.sb(ph, "a2", [128, T], F32)
            kf = self.sb(ph, "kf", [128, T], F32)
            ki = self.sb(ph, "ki", [128, T], I32)
            mk = self.sb(ph, "mk", [128, T], F32)
            Tk.dma(sp, self.sl_misc[3], posi[:], self.d_pos.partition_broadcast(128), writes=["posi"])
            self.V(lambda: dve.h.tensor_copy(ang[:], posi[:]), ["posi"], ["ang"])
            self.V(lambda: dve.h.tensor_scalar(ang[:], ang[:], self.smalls[:, C_IF:C_IF + 1], None, op0=ALU.mult),
                   ["ang", "smalls"], ["ang"])
            TWO_PI = 2.0 * math.pi
            C1 = 6.28125
            C2 = TWO_PI - C1
            for which, dst in ((0, self.sinT), (1, self.cosT)):
                if which == 1:
                    self.V(lambda: dve.h.tensor_scalar(a2[:], ang[:], math.pi / 2, None, op0=ALU.add), ["ang"], ["a2"])
                else:
                    self.V(lambda: dve.h.tensor_copy(a2[:], ang[:]), ["ang"], ["a2"])
                self.V(lambda: dve.h.tensor_scalar(kf[:], a2[:], 1.0 / TWO_PI, None, op0=ALU.mult), ["a2"], ["kf"])
                self.V(lambda: dve.h.tensor_copy(ki[:], kf[:]), ["kf"], ["ki"])
                self.V(lambda: dve.h.tensor_copy(kf[:], ki[:]), ["ki"], ["kf"])
                self.V(lambda: dve.h.scalar_tensor_tensor(a2[:], kf[:], -C1, a2[:], op0=ALU.mult, op1=ALU.add),
                       ["kf", "a2"], ["a2"])
                self.V(lambda: dve.h.scalar_tensor_tensor(a2[:], kf[:], -C2, a2[:], op0=ALU.mult, op1=ALU.add),
                       ["kf", "a2"], ["a2"])
                self.V(lambda: dve.h.tensor_single_scalar(mk[:], a2[:], math.pi, op=ALU.is_gt), ["a2"], ["mk"])
                self.V(lambda: dve.h.scalar_tensor_tensor(a2[:], mk[:], -TWO_PI, a2[:], op0=ALU.mult, op1=ALU.add),
                       ["mk", "a2"], ["a2"])
                self.V(lambda: dve.h.tensor_single_scalar(mk[:], a2[:], -math.pi, op=ALU.is_lt), ["a2"], ["mk"])
                self.V(lambda: dve.h.scalar_tensor_tensor(a2[:], mk[:], TWO_PI, a2[:], op0=ALU.mult, op1=ALU.add),
                       ["mk", "a2"], ["a2"])
                self.V(lambda: dve.h.tensor_scalar(a2[:], a2[:], 3.1415925, -3.1415925, op0=ALU.min, op1=ALU.max),
                       ["a2"], ["a2"])
                self.A(dst[:], a2[:], AF.Sin, ["a2"], ["trig%d" % which])
            self.V(lambda: dve.h.tensor_scalar(self.sinT[:], self.sinT[:], self.smalls[:, C_SG:C_SG + 1], None,
                                               op0=ALU.mult), ["trig0", "smalls"], ["trig0"])
            self.A(self.esink[:], self.smalls[:, C_SK:C_SK + 8], AF.Exp, ["smalls"], ["esink"])
            self.V(lambda: dve.h.memset(self.lb[:, 0:8], 0.0), [], ["lb0"])
            self.V(lambda: dve.h.tensor_tensor(self.lb[:, 8:16], self.smalls[:, C_LB + 8:C_LB + 16],
                                               self.smalls[:, C_LB:C_LB + 8], op=ALU.subtract), ["smalls"], ["lb1"])
            self.A(self.lb[:, 8:16], self.lb[:, 8:16], AF.Sigmoid, ["lb1"], ["lb1"])
            self.V(lambda: dve.h.tensor_scalar(self.oml[:], self.lb[:], -1.0, 1.0, op0=ALU.mult, op1=ALU.add),
                   ["lb0", "lb1"], ["oml"])
            self.barrier()

    def load_x(self, s):
        for kc in range(KC):
            self.T.dma(self.sp, self.sl_xin[kc], self.xT[:, kc, :], self.d_x[s, kc],
                       writes=[("x", kc, n) for n in range(NCB)])

    def store_x(self, s):
        for kc in range(KC):
            self.T.dma(self.sp, self.sl_xout[kc], self.d_y[s, kc], self.xT[:, kc, :],
                       reads=[("x", kc, n) for n in range(NCB)])

    def norm_block(self, n, gcol, dst, hkey, sq, rstd, bank, rkey="rstd"):
        dve = self.dve
        cols = slice(n * CB, (n + 1) * CB)
        cb = self.cb
        nsq = sq.shape[1]
        for kc in range(KC):
            self.A(sq[:, kc % nsq, :], self.xT[:, kc, cols], AF.Square, [("x", kc, n)], [("sq", kc % nsq)])
            self.mmg([("sq", kc % nsq), "cb"], [("ps", bank)],
                     [(self.ps[bank][:], cb[:, B_OD:B_OD + 128], sq[:, kc % nsq, :], kc == 0, kc == KC - 1, {})])
        self.A(rstd[:], self.ps[bank][:], AF.Ln, [("ps", bank)], [rkey], bias=EPS, scale=1.0)
        self.A(rstd[:], rstd[:], AF.Exp, [rkey], [rkey], scale=-0.5)
        for kc in range(KC):
            self.V(lambda kc=kc: dve.h.scalar_tensor_tensor(dst[:, kc, :], self.xT[:, kc, cols],
                                                            self.smalls[:, gcol + kc:gcol + kc + 1], rstd[:],
                                                            op0=ALU.mult, op1=ALU.mult),
                   [("x", kc, n), rkey, "smalls"], [(hkey, kc)])

    def ffn(self, lfs, xio=None):
        Tk, dve, pool = self.T, self.dve, self.pool
        with ExitStack() as ph:
            hT = self.sb(ph, "f_hT", [128, NCB, KC, CB], BF16)
            aT = self.sb(ph, "f_aT", [128, NJ, 2 * CB], BF16)
            gu = [self.sb(ph, f"f_gu{i}", [128, 2, KC, 128], BF16) for i in range(3)]
            wd = [self.sb(ph, f"f_wd{i}", [128, NJ, 128], BF16) for i in range(3)]
            sq = self.sb(ph, "f_sq", [128, KC, CB], BF16)
            rstd = self.sb(ph, "f_rstd", [128, CB], F32)
            sg = [self.sb(ph, f"f_sg{i}", [128, CB], F32) for i in range(2)]

            def dma_gu(li, j):
                Tk.dma(pool, self.sl_gu[j % 3], gu[j % 3][:].rearrange("p a k c -> p (a k c)"), self.d_wgu[li, j],
                       writes=[("gu", j % 3)])

            def dma_wd(li, m):
                Tk.dma(pool, self.sl_wd[m % 3], wd[m % 3][:].rearrange("p k c -> p (k c)"), self.d_wd[li, m],
                       writes=[("wd", m % 3)])

            def gcol_of(l, f):
                return C_NG + (l * 3 + (0 if f == 0 else 2)) * 8

            cnt = 0
            for idx, (l, f) in enumerate(lfs):
                li = l * 2 + f
                gcol = gcol_of(l, f)
                nxt = lfs[idx + 1] if idx + 1 < len(lfs) else None
                last = nxt is None
                hoisted = idx > 0
                for half in range(2):
                    if not (hoisted and half == 0):
                        dma_gu(li, 0)
                        dma_gu(li, 1)
                    if half == 0 and not hoisted:
                        self.norm_block(0, gcol, hT[:, 0], ("h", 0), sq, rstd, 6)
                    for j in range(NJ):
                        if j + 2 < NJ:
                            dma_gu(li, j + 2)
                        if j == NJ - 3:
                            dma_wd(li, 0)
                        if j == NJ - 2:
                            dma_wd(li, 1)
                        w = gu[j % 3]
                        for nn in range(2):
                            if half == 0 and j == 0 and nn == 1 and not hoisted:
                                self.norm_block(1, gcol, hT[:, 1], ("h", 1), sq, rstd, 6)
                            bg, bu = cnt % 2, 2 + cnt % 2
                            sgi = sg[cnt % 2]
                            sgk = ("sg", cnt % 2)
                            cnt += 1
                            hn = 2 * half + nn
                            hk = [(("h", hn), kc) for kc in range(KC)]
                            self.mmg(hk + [("gu", j % 3)], [("ps", bg)],
                                     [(self.ps[bg][:], w[:, 0, kc, :], hT[:, hn, kc, :], kc == 0, kc == KC - 1, {})
                                      for kc in range(KC)])
                            self.mmg(hk + [("gu", j % 3)], [("ps", bu)],
                                     [(self.ps[bu][:], w[:, 1, kc, :], hT[:, hn, kc, :], kc == 0, kc == KC - 1, {})
                                      for kc in range(KC)])
                            self.A(sgi[:], self.ps[bg][:], AF.Silu, [("ps", bg)], [sgk])
                            self.V(lambda sgi=sgi, bu=bu, j=j, nn=nn: dve.h.tensor_tensor(
                                aT[:, j, nn * CB:(nn + 1) * CB], sgi[:], self.ps[bu][:], op=ALU.mult),
                                [sgk, ("ps", bu)], [("a", j, nn)])
                    if half == 1 and nxt is not None:
                        dma_gu(nxt[0] * 2 + nxt[1], 0)
                        dma_gu(nxt[0] * 2 + nxt[1], 1)
                    for m in range(KC):
                        if m + 2 < KC:
                            dma_wd(li, m + 2)
                        w = wd[m % 3]
                        for nn in range(2):
                            n = 2 * half + nn
                            bd = 4 + cnt % 2
                            cnt += 1
                            cols = slice(n * CB, (n + 1) * CB)
                            self.mmg([("a", j, nn) for j in range(NJ)] + [("wd", m % 3)], [("ps", bd)],
                                     [(self.ps[bd][:], w[:, j, :], aT[:, j, nn * CB:(nn + 1) * CB], j == 0, j == NJ - 1, {})
                                      for j in range(NJ)])
                            self.V(lambda bd=bd, m=m, cols=cols: dve.h.scalar_tensor_tensor(
                                self.xT[:, m, cols], self.ps[bd][:], 0.5, self.xT[:, m, cols],
                                op0=ALU.mult, op1=ALU.add), [("ps", bd), ("x", m, n)], [("x", m, n)])
                        if half == 0 and m < 2:
                            self.norm_block(2 + m, gcol, hT[:, 2 + m], ("h", 2 + m), sq, rstd, 6)
                        if half == 1 and nxt is not None and m in (4, 5):
                            pass
                        if half == 1 and last and xio is not None:
                            s_, s_next = xio
                            self.T.dma(self.sp, self.sl_xout[m], self.d_y[s_, m], self.xT[:, m, :],
                                       reads=[("x", m, n_) for n_ in range(NCB)])
                            if s_next is not None:
                                self.T.dma(self.sp, self.sl_xin[m], self.xT[:, m, :], self.d_x[s_next, m],
                                           writes=[("x", m, n_) for n_ in range(NCB)])
                    if half == 1 and nxt is not None:
                        g2 = gcol_of(*nxt)
                        self.norm_block(0, g2, hT[:, 0], ("h", 0), sq, rstd, 6)
                        self.norm_block(1, g2, hT[:, 1], ("h", 1), sq, rstd, 6)
            self.barrier()

    def ab_mixer(self, l):
        Tk, dve, pool, act = self.T, self.dve, self.pool, self.act
        jl = l // 2
        gcol = C_NG + (l * 3 + 1) * 8
        cb, cf = self.cb, self.cf
        ps = self.ps
        with ExitStack() as ph:
            hT = [self.sb(ph, f"a_hT{i}", [128, KC, CB], BF16) for i in range(2)]
            wsl = [self.sb(ph, f"a_w{i}", [128, 2, KC, 128], BF16) for i in range(3)]
            wout = self.sb(ph, "a_wout", [128, KC, D], BF16)
            wpool = self.sb(ph, "a_wpool", [128, 4, 128], BF16)
            sq = self.sb(ph, "a_sq", [128, 2, CB], BF16)
            rstd = self.sb(ph, "a_rstd", [128, CB], F32)
            qT = self.sb(ph, "a_qT", [128, 4, CB], BF16)
            kT = self.sb(ph, "a_kT", [128, 2, 8 * 128], BF16)
            vpad = self.sb(ph, "a_vpad", [128, 8, 2, 192], BF16)
            aT = self.sb(ph, "a_aT", [128, 4, CB], BF16)
            pT = self.sb(ph, "a_pT", [128, 4, CB], BF16)
            dT = self.sb(ph, "a_dT", [128, 4, CB], BF16)
            U = self.sb(ph, "a_U", [128, 4, 16 + CB], F32)
            st = [self.sb(ph, f"a_st{i}", [128, 16 + CB], F32) for i in range(2)]
            t1 = [self.sb(ph, f"a_t1{i}", [128, CB], F32) for i in range(2)]
            t2 = [self.sb(ph, f"a_t2{i}", [128, CB], F32) for i in range(2)]
            sqh = [self.sb(ph, f"a_sqh{i}", [128, CB], BF16) for i in range(2)]
            rstd2 = [self.sb(ph, f"a_rstd2{i}", [128, CB], F32) for i in range(2)]
            PT = [self.sb(ph, f"a_PT{i}", [128, CB], BF16) for i in range(4)]
            rden = [self.sb(ph, f"a_rden{i}", [128, 256], F32) for i in range(2)]
            invc = self.sb(ph, "a_invc", [128, 4, 16], F32)

            Tk.dma(pool, self.sl_wo, wout[:].rearrange("p k c -> p (k c)"), self.d_wabout[jl], writes=["wout"])
            Tk.dma(pool, self.sl_wp, wpool[:].rearrange("p g c -> p (g c)"), self.d_wpool[jl], writes=["wpool"])
            self.V(lambda: dve.h.memset(vpad[:], 0.0), [], [("vp", s_) for s_ in range(8)])
            self.V(lambda: dve.h.memset(U[:], 0.0), [], [("U", g) for g in range(4)])
            for i_ in range(2):
                self.V(lambda i_=i_: dve.h.memset(st[i_][:], 0.0), [], [("st", i_)])
            for g, w_ in enumerate((2, 4, 8, 16)):
                for t in range(16):
                    val = 1.0 / min(t + 1, w_)
                    if t >= w_ - 1:
                        self.V(lambda g=g, t=t, val=val: dve.h.memset(invc[:, g, t:16], val), [], ["invc"])
                        break
                    self.V(lambda g=g, t=t, val=val: dve.h.memset(invc[:, g, t:t + 1], val), [], ["invc"])

            seq = [(n, pi) for n in range(NCB) for pi in (6, 7, 0, 1, 2, 3, 4, 5, 8)]
            slot_of = {}

            def dma_pair(i):
                n, pi = seq[i]
                k = i % 3
                slot_of[i] = k
                Tk.dma(pool, self.sl_ab[k], wsl[k][:].rearrange("p a k c -> p (a k c)"), self.d_wabin[jl, pi],
                       writes=[("abw", k)])

            def stageA(i):
                n, pi = seq[i]
                e = i % 2
                k = slot_of[i]
                w = wsl[k]
                h_ = hT[n % 2]
                hk = [(("ah", n % 2), kc) for kc in range(KC)]
                if pi < 8:
                    for a_ in range(2):
                        b_ = 2 * e + a_
                        self.mmg(hk + [("abw", k)], [("ps", b_)],
                                 [(ps[b_][:], w[:, a_, kc, :], h_[:, kc, :], kc == 0, kc == KC - 1, {})
                                  for kc in range(KC)])
                else:
                    b_ = 2 * e
                    mms = []
                    for tt in range(4):
                        for kc in range(KC):
                            mms.append((ps[b_][:, tt * 128:(tt + 1) * 128], h_[:, kc, tt * 128:(tt + 1) * 128],
                                        w[:, 0, kc, :], kc == 0, kc == KC - 1, {}))
                    self.mmg(hk + [("abw", k)], [("ps", b_)], mms)

            def stageB(i):
                n, pi = seq[i]
                e = i % 2
                cols = slice(n * CB, (n + 1) * CB)
                b0, b1 = 2 * e, 2 * e + 1
                if pi < 6:
                    bs_ = 4 + e
                    t1e, t2e, sqe, rse = t1[e], t2[e], sqh[e], rstd2[e]
                    k1, k2, ks, kr = ("t1", e), ("t2", e), ("sqh", e), ("rstd2", e)
                    self.A(sqe[:], ps[b0][:], AF.Square, [("ps", b0)], [ks])
                    self.mmg([ks, "cb"], [("ps", bs_)], [(ps[bs_][:], cb[:, B_BD:B_BD + 128], sqe[:], True, True, {})])
                    self.A(rse[:], ps[bs_][:], AF.Ln, [("ps", bs_)], [kr], bias=EPS, scale=1.0)
                    self.A(rse[:], rse[:], AF.Exp, [kr], [kr], scale=-0.5)
                    gc = C_QG + jl * 4 + (0 if pi < 4 else 2)
                    self.V(lambda: dve.h.scalar_tensor_tensor(
                        t1e[:], ps[b0][:], self.smalls[:, gc:gc + 1], self.cosT[:, cols],
                        op0=ALU.mult, op1=ALU.mult), [("ps", b0), "smalls", "trig1"], [k1])
                    self.V(lambda: dve.h.scalar_tensor_tensor(
                        t2e[:], ps[b1][:], self.smalls[:, gc + 1:gc + 2], self.sinT[:, cols],
                        op0=ALU.mult, op1=ALU.mult), [("ps", b1), "smalls", "trig0"], [k2])
                    self.V(lambda: dve.h.tensor_tensor(t1e[:], t1e[:], t2e[:], op=ALU.add), [k1, k2], [k1])
                    if pi < 4:
                        self.V(lambda: dve.h.tensor_tensor(qT[:, pi, :], t1e[:], rse[:], op=ALU.mult),
                               [k1, kr], [("qT", pi)])
                    else:
                        kvc = pi - 4
                        s0 = (4 * n) % 8
                        self.V(lambda: dve.h.tensor_tensor(kT[:, kvc, s0 * 128:(s0 + 4) * 128], t1e[:], rse[:],
                                                           op=ALU.mult),
                               [k1, kr], [("kT", kvc, s0 + tt) for tt in range(4)])
                elif pi < 8:
                    for a_ in range(2):
                        g = 2 * (pi - 6) + a_
                        b_ = 2 * e + a_
                        self.A(U[:, g, 16:16 + CB], ps[b_][:], AF.Copy, [("ps", b_)], [("U", g)])
                        wlen = (2, 4, 8, 16)[g]
                        src = U[:, g, :]
                        sh = 1
                        i2 = 0
                        rk = [("U", g)]
                        while sh < wlen:
                            dsti = st[i2 % 2]
                            self.V(lambda dsti=dsti, src=src, sh=sh: dve.h.tensor_tensor(
                                dsti[:, sh:16 + CB], src[:, sh:16 + CB], src[:, 0:16 + CB - sh], op=ALU.add),
                                rk, [("st", i2 % 2)])
                            src = dsti
                            rk = [("st", i2 % 2)]
                            sh *= 2
                            i2 += 1
                        self.V(lambda src=src, g=g, wlen=wlen: dve.h.scalar_tensor_tensor(
                            dT[:, g, :], src[:, 16:16 + CB], 1.0 / wlen, U[:, g, 16:16 + CB],
                            op0=ALU.mult, op1=ALU.subtract), rk + [("U", g)], [("dT", g)])
                        if n == 0:
                            t2e = t2[e]
                            self.V(lambda src=src, g=g, t2e=t2e: dve.h.tensor_tensor(
                                t2e[:, 0:16], src[:, 16:32], invc[:, g, :], op=ALU.mult), rk + ["invc"], [("t2", e)])
                            self.V(lambda g=g, t2e=t2e: dve.h.tensor_tensor(
                                dT[:, g, 0:16], t2e[:, 0:16], U[:, g, 16:32], op=ALU.subtract),
                                [("t2", e), ("U", g), ("dT", g)], [("dT", g)])
                        self.V(lambda g=g: dve.h.tensor_copy(U[:, g, 0:16], U[:, g, CB:CB + 16]),
                               [("U", g), ("dT", g)], [("U", g)])
                        self.mmg([("dT", g), "wpool"], [("ps", 6)],
                                 [(ps[6][:], wpool[:, g, :], dT[:, g, :], True, True, {})])
                        pc = C_PS + jl * 4 + g
                        self.A(pT[:, g, :], ps[6][:], AF.Copy, [("ps", 6), "smalls"], [("pT", g)],
                               scale=self.smalls[:, pc:pc + 1])
                else:
                    b_ = 2 * e
                    s0 = (4 * n) % 8
                    for tt in range(4):
                        self.A(vpad[:, s0 + tt, :, 64:128],
                               ps[b_][:, tt * 128:(tt + 1) * 128].rearrange("p (a d) -> p a d", a=2),
                               AF.Copy, [("ps", b_)], [("vp", s0 + tt)])

            def attn_S(n, u):
                qb, kv = u // 2, u % 2
                b = 4 * n + qb
                qc = slice(qb * 128, (qb + 1) * 128)
                ue = u % 2
                tiles = ([(b - 1, B_MP)] if b > 0 else []) + [(b, B_MO)]
                for ti, (kb, mcol) in enumerate(tiles):
                    ksl = kb % 8
                    bs = 2 * ue + ti
                    mms = []
                    for hi in range(4):
                        c = 2 * kv + hi // 2
                        half = hi % 2
                        prt = slice(64 * half, 64 * half + 64)
                        o = ps[bs][:, hi * 128:(hi + 1) * 128]
                        mms.append((o, kT[prt, kv, ksl * 128:(ksl + 1) * 128], qT[prt, c, qc], True, False, {}))
                        mms.append((o, cb[:, B_ID:B_ID + 128], cb[:, mcol:mcol + 128], False, True, {}))
                    self.mmg([("kT", kv, ksl), ("qT", 2 * kv), ("qT", 2 * kv + 1), "cb"], [("ps", bs)], mms)
                    self.A(PT[bs][:], ps[bs][:], AF.Exp, [("ps", bs)], [("PT", bs)], scale=0.125)

            def attn_P(n, u):
                qb, kv = u // 2, u % 2
                b = 4 * n + qb
                qc = slice(qb * 128, (qb + 1) * 128)
                ue = u % 2
                po, pd = ps[4 + 2 * ue], ps[5 + 2 * ue]
                kpo, kpd = ("ps", 4 + 2 * ue), ("ps", 5 + 2 * ue)
                tiles = ([b - 1] if b > 0 else []) + [b]
                for ti, kb in enumerate(tiles):
                    ksl = kb % 8
                    bs = 2 * ue + ti
                    mms = []
                    first, last = ti == 0, ti == len(tiles) - 1
                    for pr in range(2):
                        for half in range(2):
                            hi = 2 * pr + half
                            vv = vpad[:, ksl, kv, 64:192] if half == 0 else vpad[:, ksl, kv, 0:128]
                            oo = cb[:, B_AB + 64:B_AB + 192] if half == 0 else cb[:, B_AB:B_AB + 128]
                            rhs = PT[bs][:, hi * 128:(hi + 1) * 128]
                            st_ = first and half == 0 and pr == 0
                            sp_ = last and half == 1
                            mms.append((po[:, pr * 128:(pr + 1) * 128], vv, rhs, st_, sp_, {"skip_group_check": True}))
                            mms.append((pd[:, pr * 128:(pr + 1) * 128], oo, rhs, st_, sp_, {"skip_group_check": True}))
                    self.mmg([("PT", bs), ("vp", ksl), "cb"], [kpo, kpd], mms)
                rd = rden[ue]
                krd = ("rden", ue)
                for pr in range(2):
                    sc = jl * 4 + kv * 2 + pr
                    self.V(lambda pr=pr, sc=sc: dve.h.tensor_scalar(
                        rd[:, pr * 128:(pr + 1) * 128], pd[:, pr * 128:(pr + 1) * 128],
                        self.esink[:, sc:sc + 1], None, op0=ALU.add), [kpd, "esink"], [(krd, pr)])
                if True:
                    self.V(lambda: dve.h.reciprocal(rd[:], rd[:]), [(krd, 0), (krd, 1)], [(krd, 0), (krd, 1)])
                else:
                    self.A(rd[:], rd[:], AF.Ln, [(krd, 0), (krd, 1)], [(krd, 0), (krd, 1)])
                    self.A(rd[:], rd[:], AF.Exp, [(krd, 0), (krd, 1)], [(krd, 0), (krd, 1)], scale=-1.0)
                self.V(lambda: dve.h.tensor_tensor(
                    aT[:, 2 * kv:2 * kv + 2, qc], po[:, 0:256].rearrange("p (a q) -> p a q", a=2),
                    rd[:].rearrange("p (a q) -> p a q", a=2), op=ALU.mult),
                    [kpo, (krd, 0), (krd, 1)], [("aT", 2 * kv, qb), ("aT", 2 * kv + 1, qb)])

            def wout_stage(n):
                cols = slice(n * CB, (n + 1) * CB)
                for m in range(KC):
                    bd = m % 4
                    mms = []
                    for kc in range(KC):
                        rhs = aT[:, kc, :] if kc < 4 else pT[:, kc - 4, :]
                        mms.append((ps[bd][:], wout[:, kc, m * 128:(m + 1) * 128], rhs, kc == 0, kc == KC - 1, {}))
                    self.mmg([("aT", c, qb) for c in range(4) for qb in range(4)] + [("pT", g) for g in range(4)]
                             + ["wout"], [("ps", bd)], mms)
                    self.V(lambda bd=bd, m=m: dve.h.tensor_tensor(
                        self.xT[:, m, cols], ps[bd][:], self.xT[:, m, cols], op=ALU.add),
                        [("ps", bd), ("x", m, n)], [("x", m, n)])

            dma_pair(0)
            dma_pair(1)
            self.norm_block(0, gcol, hT[0], ("ah", 0), sq, rstd, 7)
            _skew = os.environ.get("AB_NOSKEW", "0") != "1"
            if _skew:
                stageA(0)
            _stop = int(os.environ.get("AB_STOP", "999"))
            for i, (n, pi) in enumerate(seq):
                if i >= _stop:
                    break
                if i + 2 < len(seq):
                    dma_pair(i + 2)
                if not _skew:
                    stageA(i)
                elif pi < 8:
                    stageA(i + 1)
                _hoist = os.environ.get("AB_NOHOIST", "0") != "1"
                if _hoist and pi == 3 and n + 1 < NCB:
                    self.norm_block(n + 1, gcol, hT[(n + 1) % 2], ("ah", (n + 1) % 2), sq, rstd, 7)
                if not _hoist and pi == 0 and n > 0:
                    self.norm_block(n, gcol, hT[n % 2], ("ah", n % 2), sq, rstd, 7)
                stageB(i)
                if pi == 8:
                    _off = os.environ.get("AB_OFF", "")
                    if "attn" not in _off:
                        attn_S(n, 0)
                        for u in range(8):
                            if u + 1 < 8:
                                attn_S(n, u + 1)
                            attn_P(n, u)
                    else:
                        self.V(lambda: dve.h.memset(aT[:], 0.0), [], [("aT", c, qb) for c in range(4) for qb in range(4)])
                    if "wout" not in _off:
                        wout_stage(n)
                    if i + 1 < len(seq) and _skew:
                        stageA(i + 1)
            self.barrier()

    def c_mixer(self, l):
        Tk, dve, pool, act = self.T, self.dve, self.pool, self.act
        jl = l // 2
        gcol = C_NG + (l * 3 + 1) * 8
        cb, cf = self.cb, self.cf
        ps = self.ps
        with ExitStack() as ph:
            hT = self.sb(ph, "c_hT", [128, KC, CB], BF16)
            wsl = [self.sb(ph, f"c_w{i}", [128, 4, KC, 128], BF16) for i in range(2)]
            wout = self.sb(ph, "c_wout", [128, KC, D], BF16)
            sq = self.sb(ph, "c_sq", [128, 2, CB], BF16)
            rstd = self.sb(ph, "c_rstd", [128, CB], F32)
            qo = self.sb(ph, "c_qo", [128, 8, CB], BF16)
            kt = self.sb(ph, "c_kt", [128, 8, CB], BF16)
            khtok = self.sb(ph, "c_khtok", [128, 8, 4, 128], BF16)
            vtok = self.sb(ph, "c_vtok", [128, 8, 4, 128], BF16)
            gate = self.sb(ph, "c_gate", [128, 8, CB], BF16)
            dec = self.sb(ph, "c_dec", [128, 8, 16], F32)
            S32 = self.sb(ph, "c_S32", [128, 8, 128], F32)
            Sbf = self.sb(ph, "c_Sbf", [128, 8, 128], BF16)
            Am = [self.sb(ph, f"c_Am{i}", [128, 8, 128], BF16) for i in range(2)]
            tu = [self.sb(ph, f"c_tu{i}", [128, CB], F32) for i in range(2)]
            tg = [self.sb(ph, f"c_tg{i}", [128, CB], F32) for i in range(2)]
            tq = [self.sb(ph, f"c_tq{i}", [128, CB], F32) for i in range(2)]
            te = [self.sb(ph, f"c_te{i}", [128, CB], F32) for i in range(2)]
            tn = [self.sb(ph, f"c_tn{i}", [128, CB], F32) for i in range(2)]
            sqh2 = [self.sb(ph, f"c_sqh{i}", [128, CB], BF16) for i in range(2)]
            hcol = self.sb(ph, "c_hcol", [128, 17], F32)

            lb0 = jl * 8
            self.V(lambda: dve.h.tensor_scalar(hcol[:, 0:8], self.oml[:, lb0:lb0 + 8], 0.5, None, op0=ALU.mult),
                   ["oml"], ["hcol"])
            self.V(lambda: dve.h.tensor_tensor(hcol[:, 8:16], hcol[:, 0:8], self.lb[:, lb0:lb0 + 8], op=ALU.add),
                   ["hcol", "lb0", "lb1"], ["hcol"])
            self.V(lambda: dve.h.tensor_scalar(hcol[:, 16:17], self.smalls[:, C_OG + jl:C_OG + jl + 1], 0.5, None,
                                               op0=ALU.mult), ["smalls", "hcol"], ["hcol"])
            Tk.dma(pool, self.sl_wo, wout[:].rearrange("p k c -> p (k c)"), self.d_wcout[jl], writes=["cwout"])
            self.V(lambda: dve.h.memset(S32[:], 0.0), [], [("S32", h) for h in range(8)])
            self.V(lambda: dve.h.memset(Sbf[:], 0.0), [], [("Sbf", h) for h in range(8)])

            seq = [(n, h) for n in range(NCB) for h in range(8)]

            def dma_w(i):
                n, h = seq[i]
                k = i % 2
                Tk.dma(pool, self.sl_hw[k], wsl[k][:].rearrange("p a k c -> p (a k c)"), self.d_wcin[jl, h],
                       writes=[("cw", k)])

            def stageA(i):
                n, h = seq[i]
                e = i % 2
                k = i % 2
                w = wsl[k]
                hk = [("ch", kc) for kc in range(KC)]
                for a_ in range(3):
                    b_ = 3 * e + a_
                    self.mmg(hk + [("cw", k)], [("ps", b_)],
                             [(ps[b_][:], w[:, a_, kc, :], hT[:, kc, :], kc == 0, kc == KC - 1, {})
                              for kc in range(KC)])
                mms = []
                for tt in range(4):
                    for kc in range(KC):
                        mms.append((ps[6][:, tt * 128:(tt + 1) * 128], hT[:, kc, tt * 128:(tt + 1) * 128],
                                    w[:, 3, kc, :], kc == 0, kc == KC - 1, {}))
                self.mmg(hk + [("cw", k)], [("ps", 6)], mms)

            def vcopy(i):
                n, h = seq[i]
                self.V(lambda: dve.h.tensor_copy(vtok[:, h, :, :], ps[6][:].rearrange("p (a d) -> p a d", a=4)),
                       [("ps", 6)], [("vtok", h)])

            def stageB(i):
                n, h = seq[i]
                e = i % 2
                bq, bf_, bg_ = 3 * e, 3 * e + 1, 3 * e + 2
                tue, tge, tqe, tee, tne = tu[e], tg[e], tq[e], te[e], tn[e]
                ku, kg, kq, ke, kn = ("tu", e), ("tg", e), ("tq", e), ("te", e), ("tn", e)
                self.A(tue[:], ps[bf_][:], AF.Tanh, [("ps", bf_)], [ku], scale=0.5)
                self.A(gate[:, h, :], ps[bg_][:], AF.Tanh, [("ps", bg_)], [("gate", h)], scale=0.5)
                self.A(tqe[:], ps[bq][:], AF.Silu, [("ps", bq)], [kq])
                self.V(lambda: dve.h.tensor_scalar(tue[:], tue[:], hcol[:, h:h + 1], hcol[:, 8 + h:9 + h],
                                                   op0=ALU.mult, op1=ALU.add), [ku, "hcol"], [ku])
                self.V(lambda: dve.h.tensor_scalar(tge[:], tue[:], 1e-6, None, op0=ALU.max), [ku], [kg])
                self.V(lambda: dve.h.tensor_scalar(tue[:], tue[:], -1.0, 1.0, op0=ALU.mult, op1=ALU.add), [ku], [ku])
                self.A(tge[:], tge[:], AF.Ln, [kg], [kg])
                self.V(lambda: dve.h.tensor_tensor_scan(tge[:], cf[:, F_RM:F_RM + CB], tge[:], 0.0,
                                                        ALU.mult, ALU.add), [kg, "cf"], [kg])
                self.A(tee[:], tge[:], AF.Exp, [kg], [ke])
                self.A(tne[:], tge[:], AF.Exp, [kg], [kn], scale=-1.0)
                self.V(lambda: dve.h.tensor_tensor(qo[:, h, :], tqe[:], tee[:], op=ALU.mult),
                       [kq, ke], [("qo", h, tt) for tt in range(4)])
                self.V(lambda: dve.h.tensor_tensor(tne[:], tne[:], tue[:], op=ALU.mult), [kn, ku], [kn])
                self.V(lambda: dve.h.tensor_copy(kt[:, h, :], tne[:]), [kn], [("kt", h)])
                tev = tee[:].rearrange("p (c k) -> p c k", k=32)
                self.V(lambda: dve.h.tensor_tensor(
                    tqe[:].rearrange("p (c k) -> p c k", k=32), tne[:].rearrange("p (c k) -> p c k", k=32),
                    tev[:, :, 31:32].to_broadcast([128, 16, 32]), op=ALU.mult), [kn, ke, kq], [kq])
                self.V(lambda: dve.h.tensor_copy(dec[:, h, :], tev[:, :, 31:32].rearrange("p c k -> p (c k)")),
                       [ke], [("dec", h)])

                def trf():
                    ins = None
                    for tt in range(4):
                        ins = self.pe.h.transpose(ps[7][:, tt * 128:(tt + 1) * 128],
                                                  tqe[:, tt * 128:(tt + 1) * 128], cf[:, F_ID:F_ID + 128])
                    return ins

                self.T.op(self.pe, trf, [kq, "cf"], [("ps", 7)])
                self.V(lambda: dve.h.tensor_copy(khtok[:, h, :, :], ps[7][:].rearrange("p (a d) -> p a d", a=4)),
                       [("ps", 7)], [("khtok", h)])

            def prologue(n, tt):
                tcs = slice(tt * 128, (tt + 1) * 128)
                am = Am[tt % 2]
                for hg in range(2):
                    bA = hg
                    hs = [4 * hg + hi for hi in range(4)]
                    self.mmg([("kt", h) for h in hs] + [("qo", h, tt) for h in hs], [("ps", bA)],
                             [(ps[bA][:, hi * 128:(hi + 1) * 128], kt[:, h, tcs], qo[:, h, tcs], True, True, {})
                              for hi, h in enumerate(hs)])
                    self.V(lambda hg=hg, bA=bA: dve.h.tensor_tensor(
                        am[:, 4 * hg:4 * hg + 4, :], ps[bA][:].rearrange("p (a t) -> p a t", a=4),
                        cf[:, F_HM:F_HM + 128].unsqueeze(1).to_broadcast([128, 4, 128]), op=ALU.mult),
                        [("ps", bA), "cf"], [("Am", tt % 2, hg)])

            def recur_tile(n, tt):
                tcs = slice(tt * 128, (tt + 1) * 128)
                am = Am[tt % 2]
                for hg in range(2):
                    bO = 2 + hg
                    hs = [4 * hg + hi for hi in range(4)]
                    self.mmg([("vtok", h) for h in hs] + [("Am", tt % 2, hg)], [("ps", bO)],
                             [(ps[bO][:, hi * 128:(hi + 1) * 128], vtok[:, h, tt, :], am[:, h, :], hi == 0, False,
                               {"skip_group_check": True}) for hi, h in enumerate(hs)])
                for r in range(4):
                    cidx = tt * 4 + r
                    prt = slice(32 * r, 32 * r + 32)
                    ccs = slice(tt * 128 + r * 32, tt * 128 + r * 32 + 32)
                    for g in range(4):
                        hs = [2 * g, 2 * g + 1]
                        bO = 2 + g // 2
                        bU = 4 + g
                        mms = []
                        for h in hs:
                            hi = h % 4
                            mms.append((ps[bO][:, hi * 128 + r * 32:hi * 128 + r * 32 + 32], Sbf[:, h, :],
                                        qo[:, h, ccs], False, True, {"skip_group_check": True}))
                        for j_, h in enumerate(hs):
                            mms.append((ps[bU][:, j_ * 128:(j_ + 1) * 128], khtok[prt, h, tt, :],
                                        vtok[prt, h, tt, :], True, True, {"tile_position": (32 * r, 0)}))
                        self.mmg([("Sbf", h) for h in hs] + [("qo", h, tt) for h in hs]
                                 + [("khtok", h) for h in hs] + [("vtok", h) for h in hs],
                                 [("ps", bO), ("ps", bU)], mms)
                        for j_, h in enumerate(hs):
                            self.V(lambda h=h, j_=j_, bU=bU, cidx=cidx: dve.h.scalar_tensor_tensor(
                                S32[:, h, :], S32[:, h, :], dec[:, h, cidx:cidx + 1],
                                ps[bU][:, j_ * 128:(j_ + 1) * 128], op0=ALU.mult, op1=ALU.add),
                                [("S32", h), ("dec", h), ("ps", bU)], [("S32", h)])
                        self.A(Sbf[:, 2 * g:2 * g + 2, :], S32[:, 2 * g:2 * g + 2, :], AF.Copy,
                               [("S32", h) for h in hs], [("Sbf", h) for h in hs])
                for hg in range(2):
                    bO = 2 + hg
                    hs = [4 * hg + hi for hi in range(4)]
                    self.V(lambda hg=hg, bO=bO: dve.h.scalar_tensor_tensor(
                        qo[:, 4 * hg:4 * hg + 4, tcs], gate[:, 4 * hg:4 * hg + 4, tcs], 1.0,
                        ps[bO][:].rearrange("p (a t) -> p a t", a=4), op0=ALU.add, op1=ALU.mult),
                        [("ps", bO)] + [("gate", h) for h in hs], [("qo", h, tt) for h in hs])

            def recur(n):
                prologue(n, 0)
                for tt in range(4):
                    if tt + 1 < 4:
                        prologue(n, tt + 1)
                    recur_tile(n, tt)

            def out_stage(n):
                cols = slice(n * CB, (n + 1) * CB)
                for h in range(8):
                    ok = [("qo", h, tt) for tt in range(4)]
                    sqx = sqh2[h % 2]
                    kx = ("csqh", h % 2)
                    if h % 2 == 0:
                        self.A(sqx[:], qo[:, h, :], AF.Square, ok, [kx])
                    else:
                        self.V(lambda h=h, sqx=sqx: dve.h.tensor_tensor(sqx[:], qo[:, h, :], qo[:, h, :], op=ALU.mult),
                               ok, [kx])
                    self.mmg([kx, "cb"], [("ps", h)], [(ps[h][:], cb[:, B_Q1:B_Q1 + 128], sqx[:], True, True, {})])
                for h in range(8):
                    self.A(ps[h][:], ps[h][:], AF.Ln, [("ps", h)], [("ps", h)], bias=EPS, scale=1.0)
                for h in range(8):
                    self.A(ps[h][:], ps[h][:], AF.Exp, [("ps", h)], [("ps", h)], scale=-0.5)
                for h in range(8):
                    ok = [("qo", h, tt) for tt in range(4)]
                    self.V(lambda h=h: dve.h.scalar_tensor_tensor(
                        qo[:, h, :], qo[:, h, :], hcol[:, 16:17], ps[h][:], op0=ALU.mult, op1=ALU.mult),
                        ok + [("ps", h), "hcol"], ok)
                for m in range(KC):
                    bd = m
                    self.mmg([("qo", h, tt) for h in range(8) for tt in range(4)] + ["cwout"], [("ps", bd)],
                             [(ps[bd][:], wout[:, kc, m * 128:(m + 1) * 128], qo[:, kc, :], kc == 0, kc == KC - 1, {})
                              for kc in range(KC)])
                    self.V(lambda bd=bd, m=m: dve.h.tensor_tensor(
                        self.xT[:, m, cols], ps[bd][:], self.xT[:, m, cols], op=ALU.add),
                        [("ps", bd), ("x", m, n)], [("x", m, n)])

            dma_w(0)
            dma_w(1)
            self.norm_block(0, gcol, hT, "ch", sq, rstd, 7)
            for i, (n, h) in enumerate(seq):
                if h == 0:
                    stageA(i)
                    vcopy(i)
                if i + 2 < len(seq):
                    dma_w(i + 2)
                if h < 7:
                    stageA(i + 1)
                stageB(i)
                if h < 7:
                    vcopy(i + 1)
                if h == 7:
                    recur(n)
                    if n + 1 < NCB:
                        self.norm_block(n + 1, gcol, hT, "ch", sq, rstd, 7)
                    out_stage(n)
            self.barrier()


def _prep_consts():
    p = np.arange(128)
    cf32 = np.zeros((128, NF), np.float32)
    cf32[:, F_ID:F_ID + 128] = np.eye(128, dtype=np.float32)
    s = p[:, None]
    t = p[None, :]
    cf32[:, F_HM:F_HM + 128] = ((s // 32 == t // 32) & (s <= t)).astype(np.float32)
    rm = np.ones(CB, np.float32)
    rm[::32] = 0.0
    cf32[:, F_RM:F_RM + CB] = rm[None, :]
    cbf = np.zeros((128, NBC), np.float32)
    cbf[:, B_ID:B_ID + 128] = np.eye(128, dtype=np.float32)
    cbf[:, B_OD:B_OD + 128] = 1.0 / 1024.0
    cbf[:, B_BD:B_BD + 128] = (s // 64 == t // 64).astype(np.float32) / 64.0
    cbf[:, B_O1:B_O1 + 128] = 1.0 / 128.0
    cbf[:, B_Q1:B_Q1 + 128] = 0.25 / 128.0
    cbf[:, B_AB + 64:B_AB + 128] = 1.0
    NEG = -30000.0
    cbf[:, B_MP:B_MP + 128] = np.where(s > t, 0.0, NEG)
    cbf[:, B_MO:B_MO + 128] = np.where(t >= s, 0.0, NEG)
    return cf32, cbf


def _prep_smalls(inp):
    p = np.arange(128)
    sm = np.zeros((128, NS), np.float32)
    ng = inp["norm_gains"]
    for l in range(DEPTH):
        for i in range(3):
            sm[:, C_NG + (l * 3 + i) * 8:C_NG + (l * 3 + i) * 8 + 8] = ng[l, i].reshape(8, 128).T
    d = p % 64
    dsw = (d + 32) % 64
    for j in range(2):
        sm[:, C_QG + j * 4 + 0] = inp["q_norm_gain"][j][d]
        sm[:, C_QG + j * 4 + 1] = inp["q_norm_gain"][j][dsw]
        sm[:, C_QG + j * 4 + 2] = inp["k_norm_gain"][j][d]
        sm[:, C_QG + j * 4 + 3] = inp["k_norm_gain"][j][dsw]
        sm[:, C_PS + j * 4:C_PS + j * 4 + 4] = inp["pool_scale"][j].reshape(4, 128).T
        sm[:, C_OG + j] = inp["c_out_norm_gain"][j]
        sm[:, C_LB + j * 8:C_LB + j * 8 + 8] = inp["lb_logits"][j].reshape(8, 128).T
        for kv in range(2):
            for pr in range(2):
                heads = 4 * kv + 2 * pr + (p >= 64).astype(np.int64)
                sm[:, C_SK + j * 4 + kv * 2 + pr] = inp["attn_sinks"][j][heads]
    half = 32
    inv_freq = (10000.0 ** (-np.arange(half, dtype=np.float32) / half)).astype(np.float32)
    sm[:, C_IF] = inv_freq[p % 32]
    sm[:, C_SG] = np.where((p % 64) < 32, -1.0, 1.0)
    return sm


def _kc_layout(w):
    return w.reshape(8, 128, w.shape[1]).transpose(1, 0, 2)


def _prep_weights(inp):
    out = {}
    wg, wu, wdn = inp["ffn_w_gate"], inp["ffn_w_up"], inp["ffn_w_down"]
    wgu = np.empty((DEPTH * 2, NJ, 128, 2, KC, 128), np.float32)
    wd = np.empty((DEPTH * 2, KC, 128, NJ, 128), np.float32)
    for l in range(DEPTH):
        for f in range(2):
            li = l * 2 + f
            g = wg[l, f].reshape(8, 128, NJ, 128)
            u = wu[l, f].reshape(8, 128, NJ, 128)
            wgu[li, :, :, 0] = g.transpose(2, 1, 0, 3)
            wgu[li, :, :, 1] = u.transpose(2, 1, 0, 3)
            dn = wdn[l, f].reshape(NJ, 128, 8, 128)
            wd[li] = dn.transpose(2, 1, 0, 3)
    out["wgu"] = wgu.reshape(DEPTH * 2, NJ, 128, 2 * KC * 128)
    out["wd"] = wd.reshape(DEPTH * 2, KC, 128, NJ * 128)
    p = np.arange(128)
    d = p % 64
    dsw = (d + 32) % 64
    wabin = np.zeros((2, 9, 128, 2, KC, 128), np.float32)
    for j in range(2):
        w = inp["ab_w_in"][j]
        for c in range(4):
            heads = 2 * c + p // 64
            wabin[j, c, :, 0] = _kc_layout(w[:, heads * 64 + d])
            wabin[j, c, :, 1] = _kc_layout(w[:, heads * 64 + dsw])
        for kv in range(2):
            wabin[j, 4 + kv, :, 0] = _kc_layout(w[:, 512 + kv * 64 + d])
            wabin[j, 4 + kv, :, 1] = _kc_layout(w[:, 512 + kv * 64 + dsw])
        for g in range(4):
            wabin[j, 6 + g // 2, :, g % 2] = _kc_layout(w[:, 768 + g * 128:768 + (g + 1) * 128])
        wabin[j, 8, :, 0] = _kc_layout(w[:, 640:768])
    out["wabin"] = wabin.reshape(2, 9, 128, 2 * KC * 128)
    out["wabout"] = np.stack([_kc_layout(inp["ab_w_out"][j]) for j in range(2)]).reshape(2, 128, KC * D)
    out["wpool"] = np.ascontiguousarray(inp["pool_w"].transpose(0, 2, 1, 3)).reshape(2, 128, 4 * 128)
    wcin = np.empty((2, 8, 128, 4, KC, 128), np.float32)
    for j in range(2):
        w = inp["c_w_in"][j]
        for h in range(8):
            for a_, off in enumerate((0, 1024, 3072, 2048)):
                wcin[j, h, :, a_] = _kc_layout(w[:, off + h * 128:off + (h + 1) * 128])
    out["wcin"] = wcin.reshape(2, 8, 128, 4 * KC * 128)
    out["wcout"] = np.stack([_kc_layout(inp["c_w_out"][j]) for j in range(2)]).reshape(2, 128, KC * D)
    return {k: np.ascontiguousarray(v, dtype=np.float32) for k, v in out.items()}


_CACHE = {}


def kernel(**inputs):
    inp = {k: np.asarray(v) for k, v in inputs.items()}
    x = inp["x"].astype(np.float32, copy=False)
    B = x.shape[0]
    n_seq = B // NCORES
    if "prog" not in _CACHE:
        _CACHE["prog"] = Prog(n_seq)
    prog = _CACHE["prog"]
    cf32, cbf = _prep_consts()
    shared = _prep_weights(inp)
    shared["smalls"] = _prep_smalls(inp)
    shared["cf32"] = cf32
    shared["cbf"] = cbf
    shared["pos"] = np.ascontiguousarray(inp["positions"].astype(np.int32).reshape(1, T))
    xfm = np.ascontiguousarray(x.reshape(B, T, KC, 128).transpose(0, 2, 3, 1))
    in_maps = []
    for c in range(NCORES):
        m = dict(shared)
        m["x"] = xfm[c * n_seq:(c + 1) * n_seq]
        in_maps.append(m)
    res = run_bass_kernel_spmd(prog.nc, in_maps, core_ids=list(range(NCORES)))
    ys = np.concatenate([np.asarray(r["y"]) for r in res.results], axis=0)
    out = np.ascontiguousarray(ys.transpose(0, 3, 1, 2)).reshape(B, T, D)
    return out.astype(np.float32, copy=False)
```

```python
import math
import numpy as np
from contextlib import ExitStack
import concourse.bass as bass
import concourse.mybir as mybir
from concourse.bass_utils import run_bass_kernel_spmd

F32 = mybir.dt.float32
BF16 = mybir.dt.bfloat16
I32 = mybir.dt.int32
AF = mybir.ActivationFunctionType
ALU = mybir.AluOpType

NCORES = 8
DEPTH = 4
D = 1024
T = 2048
DFF = 2816
NJ = 22
KC = 8
CB = 512
NCB = 4
EPS = 1e-6

C_NG = 0
C_QG = 96
C_PS = 104
C_OG = 112
C_LB = 114
C_SK = 130
C_IF = 138
C_SG = 139
NS = 140
F_ID = 0
F_HM = 128
F_RM = 256
NF = 768
B_ID = 0
B_OD = 128
B_BD = 256
B_O1 = 384
B_AB = 512
B_MP = 704
B_MO = 832
B_Q1 = 960
NBC = 1088


class Eng:
    def __init__(self, h, sem):
        self.h = h
        self.sem = sem
        self.count = 0
        self.waited = {}


class Slot:
    def __init__(self, sem):
        self.sem = sem
        self.count = 0


class Trk:
    def __init__(self):
        self.lw = {}
        self.rd = {}

    def _deps(self, eng, reads, writes):
        deps = {}

        def add(p, v):
            if deps.get(p, 0) < v:
                deps[p] = v

        for r in reads:
            t = self.lw.get(r)
            if t is not None:
                add(*t)
            if isinstance(r, tuple) and r[0] == "ps":
                for p, v in self.rd.get(r, {}).items():
                    if p is not eng:
                        add(p, v)
        for w in writes:
            t = self.lw.get(w)
            if t is not None:
                add(*t)
            for p, v in self.rd.get(w, {}).items():
                add(p, v)
        return deps

    def _wait(self, eng, deps):
        for p, v in deps.items():
            if eng.waited.get(p, 0) < v:
                eng.h.wait_ge(p.sem, v)
                eng.waited[p] = v

    def _commit(self, tok, reads, writes):
        p, v = tok
        for r in reads:
            d = self.rd.setdefault(r, {})
            if d.get(p, 0) < v:
                d[p] = v
        for w in writes:
            self.lw[w] = tok
            self.rd[w] = {}

    def op(self, eng, fn, reads=(), writes=()):
        self._wait(eng, self._deps(eng, reads, writes))
        ins = fn()
        eng.count += 1
        ins.then_inc(eng.sem, 1)
        self._commit((eng, eng.count), reads, writes)

    def dma(self, q, slot, out, in_, reads=(), writes=()):
        self._wait(q, self._deps(q, reads, writes))
        q.h.dma_start(out=out, in_=in_).then_inc(slot.sem, 16)
        slot.count += 16
        self._commit((slot, slot.count), reads, writes)


class Prog:
    def __init__(self, n_seq, parts=("ffn", "ab", "c"), depth=DEPTH):
        self.n_seq = n_seq
        self.parts = parts
        self.depth = depth
        self.nc = bass.Bass("TRN2", target_bir_lowering=False)
        self.T = Trk()
        self.slots = []
        self.build()

    def sem(self, name):
        return self.es.enter_context(self.nc.semaphore(name))

    def slot(self, name):
        s = Slot(self.sem(name))
        self.slots.append(s)
        return s

    def sb(self, es, name, shape, dt):
        self.uid = getattr(self, "uid", 0) + 1
        return es.enter_context(self.nc.sbuf_tensor(f"{name}_{self.uid}", shape, dt))

    def barrier(self):
        prods = self.engs + self.slots
        for e in self.engs:
            for p in prods:
                if p is e or p.count == 0:
                    continue
                if e.waited.get(p, 0) < p.count:
                    e.h.wait_ge(p.sem, p.count)
                    e.waited[p] = p.count

    def mmg(self, reads, writes, mms):
        pe = self.pe

        def fn():
            ins = None
            for (o, l, r, st, sp, kw) in mms:
                ins = pe.h.matmul(o, l, r, start=st, stop=sp, **kw)
            return ins

        self.T.op(pe, fn, reads, writes)

    def A(self, out, in_, func, reads, writes, **kw):
        act = self.act
        self.T.op(act, lambda: act.h.activation(out=out, in_=in_, func=func, **kw), reads, writes)

    def V(self, fn, reads, writes):
        self.T.op(self.dve, fn, reads, writes)

    def build(self):
        nc = self.nc
        ns = self.n_seq
        dr = lambda name, shape, dt=F32, kind="ExternalInput": nc.dram_tensor(name, shape, dt, kind=kind).ap()
        self.d_x = dr("x", [ns, KC, 128, T])
        self.d_y = dr("y", [ns, KC, 128, T], kind="ExternalOutput")
        self.d_pos = dr("pos", [1, T], I32)
        self.d_smalls = dr("smalls", [128, NS])
        self.d_cf32 = dr("cf32", [128, NF])
        self.d_cbf = dr("cbf", [128, NBC])
        self.d_wgu = dr("wgu", [DEPTH * 2, NJ, 128, 2 * KC * 128])
        self.d_wd = dr("wd", [DEPTH * 2, KC, 128, NJ * 128])
        self.d_wabin = dr("wabin", [2, 9, 128, 2 * KC * 128])
        self.d_wabout = dr("wabout", [2, 128, KC * D])
        self.d_wpool = dr("wpool", [2, 128, 4 * 128])
        self.d_wcin = dr("wcin", [2, 8, 128, 4 * KC * 128])
        self.d_wcout = dr("wcout", [2, 128, KC * D])

        with ExitStack() as es:
            self.es = es
            self.pe = Eng(nc.tensor, self.sem("s_pe"))
            self.act = Eng(nc.scalar, self.sem("s_act"))
            self.dve = Eng(nc.vector, self.sem("s_dve"))
            self.pool = Eng(nc.gpsimd, self.sem("s_pool"))
            self.sp = Eng(nc.sync, self.sem("s_sp"))
            self.engs = [self.pe, self.act, self.dve, self.pool, self.sp]
            self.ps = [es.enter_context(nc.psum_tensor(f"ps{i}", [128, CB], F32)) for i in range(8)]
            self.xT = self.sb(es, "xT", [128, KC, T], F32)
            self.smalls = self.sb(es, "smalls_sb", [128, NS], F32)
            self.cf = self.sb(es, "cf_sb", [128, NF], F32)
            self.cb = self.sb(es, "cb_sb", [128, NBC], BF16)
            self.cosT = self.sb(es, "cosT", [128, T], F32)
            self.sinT = self.sb(es, "sinT", [128, T], F32)
            self.esink = self.sb(es, "esink", [128, 8], F32)
            self.lb = self.sb(es, "lb", [128, 16], F32)
            self.oml = self.sb(es, "oml", [128, 16], F32)
            self.sl_gu = [self.slot(f"sl_gu{i}") for i in range(3)]
            self.sl_wd = [self.slot(f"sl_wd{i}") for i in range(3)]
            self.sl_xin = [self.slot(f"sl_xin{i}") for i in range(KC)]
            self.sl_xout = [self.slot(f"sl_xout{i}") for i in range(KC)]
            self.sl_misc = [self.slot(f"sl_misc{i}") for i in range(4)]
            self.sl_ab = [self.slot(f"sl_ab{i}") for i in range(3)]
            self.sl_wo = self.slot("sl_wo")
            self.sl_wp = self.slot("sl_wp")
            self.sl_hw = [self.slot(f"sl_hw{i}") for i in range(2)]
            self.rr = 0

            self.init_consts()
            self.load_x(0)
            for s in range(ns):
                s_next = s + 1 if s + 1 < ns else None
                for l in range(self.depth):
                    mix = (l % 2 == 0 and "ab" in self.parts) or (l % 2 == 1 and "c" in self.parts)
                    if "ffn" in self.parts and l == 0:
                        self.ffn([(0, 0)])
                    if mix:
                        if l % 2 == 0:
                            self.ab_mixer(l)
                        else:
                            self.c_mixer(l)
                    if "ffn" in self.parts:
                        if l + 1 < self.depth:
                            self.ffn([(l, 1), (l + 1, 0)])
                        else:
                            self.ffn([(l, 1)], xio=(s, s_next))
                if "ffn" not in self.parts:
                    self.store_x(s)
                    if s_next is not None:
                        self.load_x(s_next)
            sp = self.sp
            for sl in self.sl_xout:
                if sl.count > 0:
                    sp.h.wait_ge(sl.sem, sl.count)

    def init_consts(self):
        nc, Tk = self.nc, self.T
        sp, pool, dve, act = self.sp, self.pool, self.dve, self.act
        Tk.dma(sp, self.sl_misc[0], self.smalls[:], self.d_smalls, writes=["smalls"])
        Tk.dma(sp, self.sl_misc[1], self.cf[:], self.d_cf32, writes=["cf"])
        Tk.dma(pool, self.sl_misc[2], self.cb[:], self.d_cbf, writes=["cb"])
        with ExitStack() as ph:
            posi = self.sb(ph, "posi", [128, T], I32)
            ang = self.sb(ph, "ang", [128, T], F32)
            a2 = self.sb(ph, "a2", [128, T], F32)
            kf = self.sb(ph, "kf", [128, T], F32)
            ki = self.sb(ph, "ki", [128, T], I32)
            mk = self.sb(ph, "mk", [128, T], F32)
            Tk.dma(sp, self.sl_misc[3], posi[:], self.d_pos.partition_broadcast(128), writes=["posi"])
            self.V(lambda: dve.h.tensor_copy(ang[:], posi[:]), ["posi"], ["ang"])
            self.V(lambda: dve.h.tensor_scalar(ang[:], ang[:], self.smalls[:, C_IF:C_IF + 1], None, op0=ALU.mult),
                   ["ang", "smalls"], ["ang"])
            TWO_PI = 2.0 * math.pi
            C1 = 6.28125
            C2 = TWO_PI - C1
            for which, dst in ((0, self.sinT), (1, self.cosT)):
                if which == 1:
                    self.V(lambda: dve.h.tensor_scalar(a2[:], ang[:], math.pi / 2, None, op0=ALU.add), ["ang"], ["a2"])
                else:
                    self.V(lambda: dve.h.tensor_copy(a2[:], ang[:]), ["ang"], ["a2"])
                self.V(lambda: dve.h.tensor_scalar(kf[:], a2[:], 1.0 / TWO_PI, None, op0=ALU.mult), ["a2"], ["kf"])
                self.V(lambda: dve.h.tensor_copy(ki[:], kf[:]), ["kf"], ["ki"])
                self.V(lambda: dve.h.tensor_copy(kf[:], ki[:]), ["ki"], ["kf"])
                self.V(lambda: dve.h.scalar_tensor_tensor(a2[:], kf[:], -C1, a2[:], op0=ALU.mult, op1=ALU.add),
                       ["kf", "a2"], ["a2"])
                self.V(lambda: dve.h.scalar_tensor_tensor(a2[:], kf[:], -C2, a2[:], op0=ALU.mult, op1=ALU.add),
                       ["kf", "a2"], ["a2"])
                self.V(lambda: dve.h.tensor_single_scalar(mk[:], a2[:], math.pi, op=ALU.is_gt), ["a2"], ["mk"])
                self.V(lambda: dve.h.scalar_tensor_tensor(a2[:], mk[:], -TWO_PI, a2[:], op0=ALU.mult, op1=ALU.add),
                       ["mk", "a2"], ["a2"])
                self.V(lambda: dve.h.tensor_single_scalar(mk[:], a2[:], -math.pi, op=ALU.is_lt), ["a2"], ["mk"])
                self.V(lambda: dve.h.scalar_tensor_tensor(a2[:], mk[:], TWO_PI, a2[:], op0=ALU.mult, op1=ALU.add),
                       ["mk", "a2"], ["a2"])
                self.V(lambda: dve.h.tensor_scalar(a2[:], a2[:], 3.1415925, -3.1415925, op0=ALU.min, op1=ALU.max),
                       ["a2"], ["a2"])
                self.A(dst[:], a2[:], AF.Sin, ["a2"], ["trig%d" % which])
            self.V(lambda: dve.h.tensor_scalar(self.sinT[:], self.sinT[:], self.smalls[:, C_SG:C_SG + 1], None,
                                               op0=ALU.mult), ["trig0", "smalls"], ["trig0"])
            self.A(self.esink[:], self.smalls[:, C_SK:C_SK + 8], AF.Exp, ["smalls"], ["esink"])
            self.V(lambda: dve.h.memset(self.lb[:, 0:8], 0.0), [], ["lb0"])
            self.V(lambda: dve.h.tensor_tensor(self.lb[:, 8:16], self.smalls[:, C_LB + 8:C_LB + 16],
                                               self.smalls[:, C_LB:C_LB + 8], op=ALU.subtract), ["smalls"], ["lb1"])
            self.A(self.lb[:, 8:16], self.lb[:, 8:16], AF.Sigmoid, ["lb1"], ["lb1"])
            self.V(lambda: dve.h.tensor_scalar(self.oml[:], self.lb[:], -1.0, 1.0, op0=ALU.mult, op1=ALU.add),
                   ["lb0", "lb1"], ["oml"])
            self.barrier()

    def load_x(self, s):
        for kc in range(KC):
            self.T.dma(self.sp, self.sl_xin[kc], self.xT[:, kc, :], self.d_x[s, kc],
                       writes=[("x", kc, n) for n in range(NCB)])

    def store_x(self, s):
        for kc in range(KC):
            self.T.dma(self.sp, self.sl_xout[kc], self.d_y[s, kc], self.xT[:, kc, :],
                       reads=[("x", kc, n) for n in range(NCB)])

    def norm_block(self, n, gcol, dst, hkey, sq, rstd, bank, rkey="rstd"):
        dve = self.dve
        cols = slice(n * CB, (n + 1) * CB)
        cb = self.cb
        nsq = sq.shape[1]
        for kc in range(KC):
            self.A(sq[:, kc % nsq, :], self.xT[:, kc, cols], AF.Square, [("x", kc, n)], [("sq", kc % nsq)])
            self.mmg([("sq", kc % nsq), "cb"], [("ps", bank)],
                     [(self.ps[bank][:], cb[:, B_OD:B_OD + 128], sq[:, kc % nsq, :], kc == 0, kc == KC - 1, {})])
        self.A(rstd[:], self.ps[bank][:], AF.Ln, [("ps", bank)], [rkey], bias=EPS, scale=1.0)
        self.A(rstd[:], rstd[:], AF.Exp, [rkey], [rkey], scale=-0.5)
        for kc in range(KC):
            self.V(lambda kc=kc: dve.h.scalar_tensor_tensor(dst[:, kc, :], self.xT[:, kc, cols],
                                                            self.smalls[:, gcol + kc:gcol + kc + 1], rstd[:],
                                                            op0=ALU.mult, op1=ALU.mult),
                   [("x", kc, n), rkey, "smalls"], [(hkey, kc)])

    def ffn(self, lfs, xio=None):
        Tk, dve, pool = self.T, self.dve, self.pool
        with ExitStack() as ph:
            hT = self.sb(ph, "f_hT", [128, NCB, KC, CB], BF16)
            aT = self.sb(ph, "f_aT", [128, NJ, 2 * CB], BF16)
            gu = [self.sb(ph, f"f_gu{i}", [128, 2, KC, 128], BF16) for i in range(3)]
            wd = [self.sb(ph, f"f_wd{i}", [128, NJ, 128], BF16) for i in range(3)]
            sq = self.sb(ph, "f_sq", [128, KC, CB], BF16)
            rstd = self.sb(ph, "f_rstd", [128, CB], F32)
            sg = [self.sb(ph, f"f_sg{i}", [128, CB], F32) for i in range(2)]

            def dma_gu(li, j):
                Tk.dma(pool, self.sl_gu[j % 3], gu[j % 3][:].rearrange("p a k c -> p (a k c)"), self.d_wgu[li, j],
                       writes=[("gu", j % 3)])

            def dma_wd(li, m):
                Tk.dma(pool, self.sl_wd[m % 3], wd[m % 3][:].rearrange("p k c -> p (k c)"), self.d_wd[li, m],
                       writes=[("wd", m % 3)])

            def gcol_of(l, f):
                return C_NG + (l * 3 + (0 if f == 0 else 2)) * 8

            cnt = 0
            for idx, (l, f) in enumerate(lfs):
                li = l * 2 + f
                gcol = gcol_of(l, f)
                nxt = lfs[idx + 1] if idx + 1 < len(lfs) else None
                last = nxt is None
                hoisted = idx > 0
                for half in range(2):
                    if not (hoisted and half == 0):
                        dma_gu(li, 0)
                        dma_gu(li, 1)
                    if half == 0 and not hoisted:
                        self.norm_block(0, gcol, hT[:, 0], ("h", 0), sq, rstd, 6)
                    for j in range(NJ):
                        if j + 2 < NJ:
                            dma_gu(li, j + 2)
                        if j == NJ - 3:
                            dma_wd(li, 0)
                        if j == NJ - 2:
                            dma_wd(li, 1)
                        w = gu[j % 3]
                        for nn in range(2):
                            if half == 0 and j == 0 and nn == 1 and not hoisted:
                                self.norm_block(1, gcol, hT[:, 1], ("h", 1), sq, rstd, 6)
                            bg, bu = cnt % 2, 2 + cnt % 2
                            sgi = sg[cnt % 2]
                            sgk = ("sg", cnt % 2)
                            cnt += 1
                            hn = 2 * half + nn
                            hk = [(("h", hn), kc) for kc in range(KC)]
                            self.mmg(hk + [("gu", j % 3)], [("ps", bg)],
                                     [(self.ps[bg][:], w[:, 0, kc, :], hT[:, hn, kc, :], kc == 0, kc == KC - 1, {})
                                      for kc in range(KC)])
                            self.mmg(hk + [("gu", j % 3)], [("ps", bu)],
                                     [(self.ps[bu][:], w[:, 1, kc, :], hT[:, hn, kc, :], kc == 0, kc == KC - 1, {})
                                      for kc in range(KC)])
                            self.A(sgi[:], self.ps[bg][:], AF.Silu, [("ps", bg)], [sgk])
                            self.V(lambda sgi=sgi, bu=bu, j=j, nn=nn: dve.h.tensor_tensor(
                                aT[:, j, nn * CB:(nn + 1) * CB], sgi[:], self.ps[bu][:], op=ALU.mult),
                                [sgk, ("ps", bu)], [("a", j, nn)])
                    if half == 1 and nxt is not None:
                        dma_gu(nxt[0] * 2 + nxt[1], 0)
                        dma_gu(nxt[0] * 2 + nxt[1], 1)
                    for m in range(KC):
                        if m + 2 < KC:
                            dma_wd(li, m + 2)
                        w = wd[m % 3]
                        for nn in range(2):
                            n = 2 * half + nn
                            bd = 4 + cnt % 2
                            cnt += 1
                            cols = slice(n * CB, (n + 1) * CB)
                            self.mmg([("a", j, nn) for j in range(NJ)] + [("wd", m % 3)], [("ps", bd)],
                                     [(self.ps[bd][:], w[:, j, :], aT[:, j, nn * CB:(nn + 1) * CB], j == 0, j == NJ - 1, {})
                                      for j in range(NJ)])
                            self.V(lambda bd=bd, m=m, cols=cols: dve.h.scalar_tensor_tensor(
                                self.xT[:, m, cols], self.ps[bd][:], 0.5, self.xT[:, m, cols],
                                op0=ALU.mult, op1=ALU.add), [("ps", bd), ("x", m, n)], [("x", m, n)])
                        if half == 0 and m < 2:
                            self.norm_block(2 + m, gcol, hT[:, 2 + m], ("h", 2 + m), sq, rstd, 6)
                        if half == 1 and nxt is not None and m < 2:
                            self.norm_block(m, gcol_of(*nxt), hT[:, m], ("h", m), sq, rstd, 6)
                        if half == 1 and last and xio is not None:
                            s_, s_next = xio
                            self.T.dma(self.sp, self.sl_xout[m], self.d_y[s_, m], self.xT[:, m, :],
                                       reads=[("x", m, n_) for n_ in range(NCB)])
                            if s_next is not None:
                                self.T.dma(self.sp, self.sl_xin[m], self.xT[:, m, :], self.d_x[s_next, m],
                                           writes=[("x", m, n_) for n_ in range(NCB)])
            self.barrier()

    def ab_mixer(self, l):
        Tk, dve, pool, act = self.T, self.dve, self.pool, self.act
        jl = l // 2
        gcol = C_NG + (l * 3 + 1) * 8
        cb, cf = self.cb, self.cf
        ps = self.ps
        with ExitStack() as ph:
            hT = [self.sb(ph, f"a_hT{i}", [128, KC, CB], BF16) for i in range(2)]
            wsl = [self.sb(ph, f"a_w{i}", [128, 2, KC, 128], BF16) for i in range(3)]
            wout = self.sb(ph, "a_wout", [128, KC, D], BF16)
            wpool = self.sb(ph, "a_wpool", [128, 4, 128], BF16)
            sq = self.sb(ph, "a_sq", [128, 2, CB], BF16)
            rstd = self.sb(ph, "a_rstd", [128, CB], F32)
            qT = self.sb(ph, "a_qT", [128, 4, CB], BF16)
            kT = self.sb(ph, "a_kT", [128, 2, 8 * 128], BF16)
            vpad = self.sb(ph, "a_vpad", [128, 8, 2, 192], BF16)
            aT = self.sb(ph, "a_aT", [128, 4, CB], BF16)
            pT = self.sb(ph, "a_pT", [128, 4, CB], BF16)
            dT = self.sb(ph, "a_dT", [128, 4, CB], BF16)
            U = self.sb(ph, "a_U", [128, 4, 16 + CB], F32)
            st = [self.sb(ph, f"a_st{i}", [128, 16 + CB], F32) for i in range(2)]
            t1 = [self.sb(ph, f"a_t1{i}", [128, CB], F32) for i in range(2)]
            t2 = [self.sb(ph, f"a_t2{i}", [128, CB], F32) for i in range(2)]
            sqh = [self.sb(ph, f"a_sqh{i}", [128, CB], BF16) for i in range(2)]
            rstd2 = [self.sb(ph, f"a_rstd2{i}", [128, CB], F32) for i in range(2)]
            PT = [self.sb(ph, f"a_PT{i}", [128, CB], BF16) for i in range(4)]
            rden = [self.sb(ph, f"a_rden{i}", [128, 256], F32) for i in range(2)]
            invc = self.sb(ph, "a_invc", [128, 4, 16], F32)

            Tk.dma(pool, self.sl_wo, wout[:].rearrange("p k c -> p (k c)"), self.d_wabout[jl], writes=["wout"])
            Tk.dma(pool, self.sl_wp, wpool[:].rearrange("p g c -> p (g c)"), self.d_wpool[jl], writes=["wpool"])
            self.V(lambda: dve.h.memset(vpad[:], 0.0), [], [("vp", s_) for s_ in range(8)])
            self.V(lambda: dve.h.memset(U[:], 0.0), [], [("U", g) for g in range(4)])
            for i_ in range(2):
                self.V(lambda i_=i_: dve.h.memset(st[i_][:], 0.0), [], [("st", i_)])
            for g, w_ in enumerate((2, 4, 8, 16)):
                for t in range(16):
                    val = 1.0 / min(t + 1, w_)
                    if t >= w_ - 1:
                        self.V(lambda g=g, t=t, val=val: dve.h.memset(invc[:, g, t:16], val), [], ["invc"])
                        break
                    self.V(lambda g=g, t=t, val=val: dve.h.memset(invc[:, g, t:t + 1], val), [], ["invc"])

            seq = [(n, pi) for n in range(NCB) for pi in (6, 7, 0, 1, 2, 3, 4, 5, 8)]
            slot_of = {}

            def dma_pair(i):
                n, pi = seq[i]
                k = i % 3
                slot_of[i] = k
                Tk.dma(pool, self.sl_ab[k], wsl[k][:].rearrange("p a k c -> p (a k c)"), self.d_wabin[jl, pi],
                       writes=[("abw", k)])

            def stageA(i):
                n, pi = seq[i]
                e = i % 2
                k = slot_of[i]
                w = wsl[k]
                h_ = hT[n % 2]
                hk = [(("ah", n % 2), kc) for kc in range(KC)]
                if pi < 8:
                    for a_ in range(2):
                        b_ = 2 * e + a_
                        self.mmg(hk + [("abw", k)], [("ps", b_)],
                                 [(ps[b_][:], w[:, a_, kc, :], h_[:, kc, :], kc == 0, kc == KC - 1, {})
                                  for kc in range(KC)])
                else:
                    b_ = 2 * e
                    mms = []
                    for tt in range(4):
                        for kc in range(KC):
                            mms.append((ps[b_][:, tt * 128:(tt + 1) * 128], h_[:, kc, tt * 128:(tt + 1) * 128],
                                        w[:, 0, kc, :], kc == 0, kc == KC - 1, {}))
                    self.mmg(hk + [("abw", k)], [("ps", b_)], mms)

            def stageB(i):
                n, pi = seq[i]
                e = i % 2
                cols = slice(n * CB, (n + 1) * CB)
                b0, b1 = 2 * e, 2 * e + 1
                if pi < 6:
                    bs_ = 4 + e
                    t1e, t2e, sqe, rse = t1[e], t2[e], sqh[e], rstd2[e]
                    k1, k2, ks, kr = ("t1", e), ("t2", e), ("sqh", e), ("rstd2", e)
                    self.A(sqe[:], ps[b0][:], AF.Square, [("ps", b0)], [ks])
                    self.mmg([ks, "cb"], [("ps", bs_)], [(ps[bs_][:], cb[:, B_BD:B_BD + 128], sqe[:], True, True, {})])
                    self.A(rse[:], ps[bs_][:], AF.Ln, [("ps", bs_)], [kr], bias=EPS, scale=1.0)
                    self.A(rse[:], rse[:], AF.Exp, [kr], [kr], scale=-0.5)
                    gc = C_QG + jl * 4 + (0 if pi < 4 else 2)
                    self.V(lambda: dve.h.scalar_tensor_tensor(
                        t1e[:], ps[b0][:], self.smalls[:, gc:gc + 1], self.cosT[:, cols],
                        op0=ALU.mult, op1=ALU.mult), [("ps", b0), "smalls", "trig1"], [k1])
                    self.V(lambda: dve.h.scalar_tensor_tensor(
                        t2e[:], ps[b1][:], self.smalls[:, gc + 1:gc + 2], self.sinT[:, cols],
                        op0=ALU.mult, op1=ALU.mult), [("ps", b1), "smalls", "trig0"], [k2])
                    self.V(lambda: dve.h.tensor_tensor(t1e[:], t1e[:], t2e[:], op=ALU.add), [k1, k2], [k1])
                    if pi < 4:
                        self.V(lambda: dve.h.tensor_tensor(qT[:, pi, :], t1e[:], rse[:], op=ALU.mult),
                               [k1, kr], [("qT", pi)])
                    else:
                        kvc = pi - 4
                        s0 = (4 * n) % 8
                        self.V(lambda: dve.h.tensor_tensor(kT[:, kvc, s0 * 128:(s0 + 4) * 128], t1e[:], rse[:],
                                                           op=ALU.mult),
                               [k1, kr], [("kT", kvc, s0 + tt) for tt in range(4)])
                elif pi < 8:
                    for a_ in range(2):
                        g = 2 * (pi - 6) + a_
                        b_ = 2 * e + a_
                        self.A(U[:, g, 16:16 + CB], ps[b_][:], AF.Copy, [("ps", b_)], [("U", g)])
                        wlen = (2, 4, 8, 16)[g]
                        src = U[:, g, :]
                        sh = 1
                        i2 = 0
                        rk = [("U", g)]
                        while sh < wlen:
                            dsti = st[i2 % 2]
                            self.V(lambda dsti=dsti, src=src, sh=sh: dve.h.tensor_tensor(
                                dsti[:, sh:16 + CB], src[:, sh:16 + CB], src[:, 0:16 + CB - sh], op=ALU.add),
                                rk, [("st", i2 % 2)])
                            src = dsti
                            rk = [("st", i2 % 2)]
                            sh *= 2
                            i2 += 1
                        self.V(lambda src=src, g=g, wlen=wlen: dve.h.scalar_tensor_tensor(
                            dT[:, g, :], src[:, 16:16 + CB], 1.0 / wlen, U[:, g, 16:16 + CB],
                            op0=ALU.mult, op1=ALU.subtract), rk + [("U", g)], [("dT", g)])
                        if n == 0:
                            t2e = t2[e]
                            self.V(lambda src=src, g=g, t2e=t2e: dve.h.tensor_tensor(
                                t2e[:, 0:16], src[:, 16:32], invc[:, g, :], op=ALU.mult), rk + ["invc"], [("t2", e)])
                            self.V(lambda g=g, t2e=t2e: dve.h.tensor_tensor(
                                dT[:, g, 0:16], t2e[:, 0:16], U[:, g, 16:32], op=ALU.subtract),
                                [("t2", e), ("U", g), ("dT", g)], [("dT", g)])
                        self.V(lambda g=g: dve.h.tensor_copy(U[:, g, 0:16], U[:, g, CB:CB + 16]),
                               [("U", g), ("dT", g)], [("U", g)])
                        self.mmg([("dT", g), "wpool"], [("ps", 6)],
                                 [(ps[6][:], wpool[:, g, :], dT[:, g, :], True, True, {})])
                        pc = C_PS + jl * 4 + g
                        self.A(pT[:, g, :], ps[6][:], AF.Copy, [("ps", 6), "smalls"], [("pT", g)],
                               scale=self.smalls[:, pc:pc + 1])
                else:
                    b_ = 2 * e
                    s0 = (4 * n) % 8
                    for tt in range(4):
                        self.A(vpad[:, s0 + tt, :, 64:128],
                               ps[b_][:, tt * 128:(tt + 1) * 128].rearrange("p (a d) -> p a d", a=2),
                               AF.Copy, [("ps", b_)], [("vp", s0 + tt)])

            def attn_S(n, u):
                qb, kv = u // 2, u % 2
                b = 4 * n + qb
                qc = slice(qb * 128, (qb + 1) * 128)
                ue = u % 2
                tiles = ([(b - 1, B_MP)] if b > 0 else []) + [(b, B_MO)]
                for ti, (kb, mcol) in enumerate(tiles):
                    ksl = kb % 8
                    bs = 2 * ue + ti
                    mms = []
                    for hi in range(4):
                        c = 2 * kv + hi // 2
                        half = hi % 2
                        prt = slice(64 * half, 64 * half + 64)
                        o = ps[bs][:, hi * 128:(hi + 1) * 128]
                        mms.append((o, kT[prt, kv, ksl * 128:(ksl + 1) * 128], qT[prt, c, qc], True, False, {}))
                        mms.append((o, cb[:, B_ID:B_ID + 128], cb[:, mcol:mcol + 128], False, True, {}))
                    self.mmg([("kT", kv, ksl), ("qT", 2 * kv), ("qT", 2 * kv + 1), "cb"], [("ps", bs)], mms)
                    self.A(PT[bs][:], ps[bs][:], AF.Exp, [("ps", bs)], [("PT", bs)], scale=0.125)

            def attn_P(n, u):
                qb, kv = u // 2, u % 2
                b = 4 * n + qb
                qc = slice(qb * 128, (qb + 1) * 128)
                ue = u % 2
                po, pd = ps[4 + 2 * ue], ps[5 + 2 * ue]
                kpo, kpd = ("ps", 4 + 2 * ue), ("ps", 5 + 2 * ue)
                tiles = ([b - 1] if b > 0 else []) + [b]
                for ti, kb in enumerate(tiles):
                    ksl = kb % 8
                    bs = 2 * ue + ti
                    mms = []
                    first, last = ti == 0, ti == len(tiles) - 1
                    for pr in range(2):
                        for half in range(2):
                            hi = 2 * pr + half
                            vv = vpad[:, ksl, kv, 64:192] if half == 0 else vpad[:, ksl, kv, 0:128]
                            oo = cb[:, B_AB + 64:B_AB + 192] if half == 0 else cb[:, B_AB:B_AB + 128]
                            rhs = PT[bs][:, hi * 128:(hi + 1) * 128]
                            st_ = first and half == 0 and pr == 0
                            sp_ = last and half == 1
                            mms.append((po[:, pr * 128:(pr + 1) * 128], vv, rhs, st_, sp_, {"skip_group_check": True}))
                            mms.append((pd[:, pr * 128:(pr + 1) * 128], oo, rhs, st_, sp_, {"skip_group_check": True}))
                    self.mmg([("PT", bs), ("vp", ksl), "cb"], [kpo, kpd], mms)
                rd = rden[ue]
                krd = ("rden", ue)
                for pr in range(2):
                    sc = jl * 4 + kv * 2 + pr
                    self.V(lambda pr=pr, sc=sc: dve.h.tensor_scalar(
                        rd[:, pr * 128:(pr + 1) * 128], pd[:, pr * 128:(pr + 1) * 128],
                        self.esink[:, sc:sc + 1], None, op0=ALU.add), [kpd, "esink"], [(krd, pr)])
                self.V(lambda: dve.h.reciprocal(rd[:], rd[:]), [(krd, 0), (krd, 1)], [(krd, 0), (krd, 1)])
                self.V(lambda: dve.h.tensor_tensor(
                    aT[:, 2 * kv:2 * kv + 2, qc], po[:, 0:256].rearrange("p (a q) -> p a q", a=2),
                    rd[:].rearrange("p (a q) -> p a q", a=2), op=ALU.mult),
                    [kpo, (krd, 0), (krd, 1)], [("aT", 2 * kv, qb), ("aT", 2 * kv + 1, qb)])

            def wout_stage(n):
                cols = slice(n * CB, (n + 1) * CB)
                for m in range(KC):
                    bd = m % 4
                    mms = []
                    for kc in range(KC):
                        rhs = aT[:, kc, :] if kc < 4 else pT[:, kc - 4, :]
                        mms.append((ps[bd][:], wout[:, kc, m * 128:(m + 1) * 128], rhs, kc == 0, kc == KC - 1, {}))
                    self.mmg([("aT", c, qb) for c in range(4) for qb in range(4)] + [("pT", g) for g in range(4)]
                             + ["wout"], [("ps", bd)], mms)
                    self.V(lambda bd=bd, m=m: dve.h.tensor_tensor(
                        self.xT[:, m, cols], ps[bd][:], self.xT[:, m, cols], op=ALU.add),
                        [("ps", bd), ("x", m, n)], [("x", m, n)])

            dma_pair(0)
            dma_pair(1)
            self.norm_block(0, gcol, hT[0], ("ah", 0), sq, rstd, 7)
            stageA(0)
            for i, (n, pi) in enumerate(seq):
                if i + 2 < len(seq):
                    dma_pair(i + 2)
                if pi < 8:
                    stageA(i + 1)
                if pi == 3 and n + 1 < NCB:
                    self.norm_block(n + 1, gcol, hT[(n + 1) % 2], ("ah", (n + 1) % 2), sq, rstd, 7)
                stageB(i)
                if pi == 8:
                    attn_S(n, 0)
                    for u in range(8):
                        if u + 1 < 8:
                            attn_S(n, u + 1)
                        attn_P(n, u)
                    wout_stage(n)
                    if i + 1 < len(seq):
                        stageA(i + 1)
            self.barrier()

    def c_mixer(self, l):
        Tk, dve, pool, act = self.T, self.dve, self.pool, self.act
        jl = l // 2
        gcol = C_NG + (l * 3 + 1) * 8
        cb, cf = self.cb, self.cf
        ps = self.ps
        with ExitStack() as ph:
            hT = self.sb(ph, "c_hT", [128, KC, CB], BF16)
            wsl = [self.sb(ph, f"c_w{i}", [128, 4, KC, 128], BF16) for i in range(2)]
            wout = self.sb(ph, "c_wout", [128, KC, D], BF16)
            sq = self.sb(ph, "c_sq", [128, 2, CB], BF16)
            rstd = self.sb(ph, "c_rstd", [128, CB], F32)
            qo = self.sb(ph, "c_qo", [128, 8, CB], BF16)
            kt = self.sb(ph, "c_kt", [128, 8, CB], BF16)
            khtok = self.sb(ph, "c_khtok", [128, 8, 4, 128], BF16)
            vtok = self.sb(ph, "c_vtok", [128, 8, 4, 128], BF16)
            gate = self.sb(ph, "c_gate", [128, 8, CB], BF16)
            dec = self.sb(ph, "c_dec", [128, 8, 16], F32)
            S32 = self.sb(ph, "c_S32", [128, 8, 128], F32)
            Sbf = self.sb(ph, "c_Sbf", [128, 8, 128], BF16)
            Am = [self.sb(ph, f"c_Am{i}", [128, 8, 128], BF16) for i in range(2)]
            tu = [self.sb(ph, f"c_tu{i}", [128, CB], F32) for i in range(2)]
            tg = [self.sb(ph, f"c_tg{i}", [128, CB], F32) for i in range(2)]
            tq = [self.sb(ph, f"c_tq{i}", [128, CB], F32) for i in range(2)]
            te = [self.sb(ph, f"c_te{i}", [128, CB], F32) for i in range(2)]
            tn = [self.sb(ph, f"c_tn{i}", [128, CB], F32) for i in range(2)]
            sqh2 = [self.sb(ph, f"c_sqh{i}", [128, CB], BF16) for i in range(2)]
            hcol = self.sb(ph, "c_hcol", [128, 17], F32)

            lb0 = jl * 8
            self.V(lambda: dve.h.tensor_scalar(hcol[:, 0:8], self.oml[:, lb0:lb0 + 8], 0.5, None, op0=ALU.mult),
                   ["oml"], ["hcol"])
            self.V(lambda: dve.h.tensor_tensor(hcol[:, 8:16], hcol[:, 0:8], self.lb[:, lb0:lb0 + 8], op=ALU.add),
                   ["hcol", "lb0", "lb1"], ["hcol"])
            self.V(lambda: dve.h.tensor_scalar(hcol[:, 16:17], self.smalls[:, C_OG + jl:C_OG + jl + 1], 0.5, None,
                                               op0=ALU.mult), ["smalls", "hcol"], ["hcol"])
            Tk.dma(pool, self.sl_wo, wout[:].rearrange("p k c -> p (k c)"), self.d_wcout[jl], writes=["cwout"])
            self.V(lambda: dve.h.memset(S32[:], 0.0), [], [("S32", h) for h in range(8)])
            self.V(lambda: dve.h.memset(Sbf[:], 0.0), [], [("Sbf", h) for h in range(8)])

            seq = [(n, h) for n in range(NCB) for h in range(8)]

            def dma_w(i):
                n, h = seq[i]
                k = i % 2
                Tk.dma(pool, self.sl_hw[k], wsl[k][:].rearrange("p a k c -> p (a k c)"), self.d_wcin[jl, h],
                       writes=[("cw", k)])

            def stageA(i):
                n, h = seq[i]
                e = i % 2
                k = i % 2
                w = wsl[k]
                hk = [("ch", kc) for kc in range(KC)]
                for a_ in range(3):
                    b_ = 3 * e + a_
                    self.mmg(hk + [("cw", k)], [("ps", b_)],
                             [(ps[b_][:], w[:, a_, kc, :], hT[:, kc, :], kc == 0, kc == KC - 1, {})
                              for kc in range(KC)])
                mms = []
                for tt in range(4):
                    for kc in range(KC):
                        mms.append((ps[6][:, tt * 128:(tt + 1) * 128], hT[:, kc, tt * 128:(tt + 1) * 128],
                                    w[:, 3, kc, :], kc == 0, kc == KC - 1, {}))
                self.mmg(hk + [("cw", k)], [("ps", 6)], mms)

            def vcopy(i):
                n, h = seq[i]
                self.V(lambda: dve.h.tensor_copy(vtok[:, h, :, :], ps[6][:].rearrange("p (a d) -> p a d", a=4)),
                       [("ps", 6)], [("vtok", h)])

            def stageB(i):
                n, h = seq[i]
                e = i % 2
                bq, bf_, bg_ = 3 * e, 3 * e + 1, 3 * e + 2
                tue, tge, tqe, tee, tne = tu[e], tg[e], tq[e], te[e], tn[e]
                ku, kg, kq, ke, kn = ("tu", e), ("tg", e), ("tq", e), ("te", e), ("tn", e)
                self.A(tue[:], ps[bf_][:], AF.Tanh, [("ps", bf_)], [ku], scale=0.5)
                self.A(gate[:, h, :], ps[bg_][:], AF.Tanh, [("ps", bg_)], [("gate", h)], scale=0.5)
                self.A(tqe[:], ps[bq][:], AF.Silu, [("ps", bq)], [kq])
                self.V(lambda: dve.h.tensor_scalar(tue[:], tue[:], hcol[:, h:h + 1], hcol[:, 8 + h:9 + h],
                                                   op0=ALU.mult, op1=ALU.add), [ku, "hcol"], [ku])
                self.V(lambda: dve.h.tensor_scalar(tge[:], tue[:], 1e-6, None, op0=ALU.max), [ku], [kg])
                self.V(lambda: dve.h.tensor_scalar(tue[:], tue[:], -1.0, 1.0, op0=ALU.mult, op1=ALU.add), [ku], [ku])
                self.A(tge[:], tge[:], AF.Ln, [kg], [kg])
                self.V(lambda: dve.h.tensor_tensor_scan(tge[:], cf[:, F_RM:F_RM + CB], tge[:], 0.0,
                                                        ALU.mult, ALU.add), [kg, "cf"], [kg])
                self.A(tee[:], tge[:], AF.Exp, [kg], [ke])
                self.A(tne[:], tge[:], AF.Exp, [kg], [kn], scale=-1.0)
                self.V(lambda: dve.h.tensor_tensor(qo[:, h, :], tqe[:], tee[:], op=ALU.mult),
                       [kq, ke], [("qo", h, tt) for tt in range(4)])
                self.V(lambda: dve.h.tensor_tensor(tne[:], tne[:], tue[:], op=ALU.mult), [kn, ku], [kn])
                self.V(lambda: dve.h.tensor_copy(kt[:, h, :], tne[:]), [kn], [("kt", h)])
                tev = tee[:].rearrange("p (c k) -> p c k", k=32)
                self.V(lambda: dve.h.tensor_tensor(
                    tqe[:].rearrange("p (c k) -> p c k", k=32), tne[:].rearrange("p (c k) -> p c k", k=32),
                    tev[:, :, 31:32].to_broadcast([128, 16, 32]), op=ALU.mult), [kn, ke, kq], [kq])
                self.V(lambda: dve.h.tensor_copy(dec[:, h, :], tev[:, :, 31:32].rearrange("p c k -> p (c k)")),
                       [ke], [("dec", h)])

                def trf():
                    ins = None
                    for tt in range(4):
                        ins = self.pe.h.transpose(ps[7][:, tt * 128:(tt + 1) * 128],
                                                  tqe[:, tt * 128:(tt + 1) * 128], cf[:, F_ID:F_ID + 128])
                    return ins

                self.T.op(self.pe, trf, [kq, "cf"], [("ps", 7)])
                self.V(lambda: dve.h.tensor_copy(khtok[:, h, :, :], ps[7][:].rearrange("p (a d) -> p a d", a=4)),
                       [("ps", 7)], [("khtok", h)])

            def prologue(n, tt):
                tcs = slice(tt * 128, (tt + 1) * 128)
                am = Am[tt % 2]
                for hg in range(2):
                    bA = hg
                    hs = [4 * hg + hi for hi in range(4)]
                    self.mmg([("kt", h) for h in hs] + [("qo", h, tt) for h in hs], [("ps", bA)],
                             [(ps[bA][:, hi * 128:(hi + 1) * 128], kt[:, h, tcs], qo[:, h, tcs], True, True, {})
                              for hi, h in enumerate(hs)])
                    self.V(lambda hg=hg, bA=bA: dve.h.tensor_tensor(
                        am[:, 4 * hg:4 * hg + 4, :], ps[bA][:].rearrange("p (a t) -> p a t", a=4),
                        cf[:, F_HM:F_HM + 128].unsqueeze(1).to_broadcast([128, 4, 128]), op=ALU.mult),
                        [("ps", bA), "cf"], [("Am", tt % 2, hg)])

            def recur_tile(n, tt):
                tcs = slice(tt * 128, (tt + 1) * 128)
                am = Am[tt % 2]
                for hg in range(2):
                    bO = 2 + hg
                    hs = [4 * hg + hi for hi in range(4)]
                    self.mmg([("vtok", h) for h in hs] + [("Am", tt % 2, hg)], [("ps", bO)],
                             [(ps[bO][:, hi * 128:(hi + 1) * 128], vtok[:, h, tt, :], am[:, h, :], hi == 0, False,
                               {"skip_group_check": True}) for hi, h in enumerate(hs)])
                for r in range(4):
                    cidx = tt * 4 + r
                    prt = slice(32 * r, 32 * r + 32)
                    ccs = slice(tt * 128 + r * 32, tt * 128 + r * 32 + 32)
                    for g in range(4):
                        hs = [2 * g, 2 * g + 1]
                        bO = 2 + g // 2
                        bU = 4 + g
                        mms = []
                        for h in hs:
                            hi = h % 4
                            mms.append((ps[bO][:, hi * 128 + r * 32:hi * 128 + r * 32 + 32], Sbf[:, h, :],
                                        qo[:, h, ccs], False, True, {"skip_group_check": True}))
                        for j_, h in enumerate(hs):
                            mms.append((ps[bU][:, j_ * 128:(j_ + 1) * 128], khtok[prt, h, tt, :],
                                        vtok[prt, h, tt, :], True, True, {"tile_position": (32 * r, 0)}))
                        self.mmg([("Sbf", h) for h in hs] + [("qo", h, tt) for h in hs]
                                 + [("khtok", h) for h in hs] + [("vtok", h) for h in hs],
                                 [("ps", bO), ("ps", bU)], mms)
                        for j_, h in enumerate(hs):
                            self.V(lambda h=h, j_=j_, bU=bU, cidx=cidx: dve.h.scalar_tensor_tensor(
                                S32[:, h, :], S32[:, h, :], dec[:, h, cidx:cidx + 1],
                                ps[bU][:, j_ * 128:(j_ + 1) * 128], op0=ALU.mult, op1=ALU.add),
                                [("S32", h), ("dec", h), ("ps", bU)], [("S32", h)])
                        self.A(Sbf[:, 2 * g:2 * g + 2, :], S32[:, 2 * g:2 * g + 2, :], AF.Copy,
                               [("S32", h) for h in hs], [("Sbf", h) for h in hs])
                for hg in range(2):
                    bO = 2 + hg
                    hs = [4 * hg + hi for hi in range(4)]
                    self.V(lambda hg=hg, bO=bO: dve.h.scalar_tensor_tensor(
                        qo[:, 4 * hg:4 * hg + 4, tcs], gate[:, 4 * hg:4 * hg + 4, tcs], 1.0,
                        ps[bO][:].rearrange("p (a t) -> p a t", a=4), op0=ALU.add, op1=ALU.mult),
                        [("ps", bO)] + [("gate", h) for h in hs], [("qo", h, tt) for h in hs])

            def recur(n):
                prologue(n, 0)
                for tt in range(4):
                    if tt + 1 < 4:
                        prologue(n, tt + 1)
                    recur_tile(n, tt)

            def out_stage(n):
                cols = slice(n * CB, (n + 1) * CB)
                for h in range(8):
                    ok = [("qo", h, tt) for tt in range(4)]
                    sqx = sqh2[h % 2]
                    kx = ("csqh", h % 2)
                    if h % 2 == 0:
                        self.A(sqx[:], qo[:, h, :], AF.Square, ok, [kx])
                    else:
                        self.V(lambda h=h, sqx=sqx: dve.h.tensor_tensor(sqx[:], qo[:, h, :], qo[:, h, :], op=ALU.mult),
                               ok, [kx])
                    self.mmg([kx, "cb"], [("ps", h)], [(ps[h][:], cb[:, B_Q1:B_Q1 + 128], sqx[:], True, True, {})])
                for h in range(8):
                    self.A(ps[h][:], ps[h][:], AF.Ln, [("ps", h)], [("ps", h)], bias=EPS, scale=1.0)
                for h in range(8):
                    self.A(ps[h][:], ps[h][:], AF.Exp, [("ps", h)], [("ps", h)], scale=-0.5)
                for h in range(8):
                    ok = [("qo", h, tt) for tt in range(4)]
                    self.V(lambda h=h: dve.h.scalar_tensor_tensor(
                        qo[:, h, :], qo[:, h, :], hcol[:, 16:17], ps[h][:], op0=ALU.mult, op1=ALU.mult),
                        ok + [("ps", h), "hcol"], ok)
                for m in range(KC):
                    bd = m
                    self.mmg([("qo", h, tt) for h in range(8) for tt in range(4)] + ["cwout"], [("ps", bd)],
                             [(ps[bd][:], wout[:, kc, m * 128:(m + 1) * 128], qo[:, kc, :], kc == 0, kc == KC - 1, {})
                              for kc in range(KC)])
                    self.V(lambda bd=bd, m=m: dve.h.tensor_tensor(
                        self.xT[:, m, cols], ps[bd][:], self.xT[:, m, cols], op=ALU.add),
                        [("ps", bd), ("x", m, n)], [("x", m, n)])

            dma_w(0)
            dma_w(1)
            self.norm_block(0, gcol, hT, "ch", sq, rstd, 7)
            for i, (n, h) in enumerate(seq):
                if h == 0:
                    stageA(i)
                    vcopy(i)
                if i + 2 < len(seq):
                    dma_w(i + 2)
                if h < 7:
                    stageA(i + 1)
                stageB(i)
                if h < 7:
                    vcopy(i + 1)
                if h == 7:
                    recur(n)
                    if n + 1 < NCB:
                        self.norm_block(n + 1, gcol, hT, "ch", sq, rstd, 7)
                    out_stage(n)
            self.barrier()


def _prep_consts():
    p = np.arange(128)
    cf32 = np.zeros((128, NF), np.float32)
    cf32[:, F_ID:F_ID + 128] = np.eye(128, dtype=np.float32)
    s = p[:, None]
    t = p[None, :]
    cf32[:, F_HM:F_HM + 128] = ((s // 32 == t // 32) & (s <= t)).astype(np.float32)
    rm = np.ones(CB, np.float32)
    rm[::32] = 0.0
    cf32[:, F_RM:F_RM + CB] = rm[None, :]
    cbf = np.zeros((128, NBC), np.float32)
    cbf[:, B_ID:B_ID + 128] = np.eye(128, dtype=np.float32)
    cbf[:, B_OD:B_OD + 128] = 1.0 / 1024.0
    cbf[:, B_BD:B_BD + 128] = (s // 64 == t // 64).astype(np.float32) / 64.0
    cbf[:, B_O1:B_O1 + 128] = 1.0 / 128.0
    cbf[:, B_Q1:B_Q1 + 128] = 0.25 / 128.0
    cbf[:, B_AB + 64:B_AB + 128] = 1.0
    NEG = -30000.0
    cbf[:, B_MP:B_MP + 128] = np.where(s > t, 0.0, NEG)
    cbf[:, B_MO:B_MO + 128] = np.where(t >= s, 0.0, NEG)
    return cf32, cbf


def _prep_smalls(inp):
    p = np.arange(128)
    sm = np.zeros((128, NS), np.float32)
    ng = inp["norm_gains"]
    for l in range(DEPTH):
        for i in range(3):
            sm[:, C_NG + (l * 3 + i) * 8:C_NG + (l * 3 + i) * 8 + 8] = ng[l, i].reshape(8, 128).T
    d = p % 64
    dsw = (d + 32) % 64
    for j in range(2):
        sm[:, C_QG + j * 4 + 0] = inp["q_norm_gain"][j][d]
        sm[:, C_QG + j * 4 + 1] = inp["q_norm_gain"][j][dsw]
        sm[:, C_QG + j * 4 + 2] = inp["k_norm_gain"][j][d]
        sm[:, C_QG + j * 4 + 3] = inp["k_norm_gain"][j][dsw]
        sm[:, C_PS + j * 4:C_PS + j * 4 + 4] = inp["pool_scale"][j].reshape(4, 128).T
        sm[:, C_OG + j] = inp["c_out_norm_gain"][j]
        sm[:, C_LB + j * 8:C_LB + j * 8 + 8] = inp["lb_logits"][j].reshape(8, 128).T
        for kv in range(2):
            for pr in range(2):
                heads = 4 * kv + 2 * pr + (p >= 64).astype(np.int64)
                sm[:, C_SK + j * 4 + kv * 2 + pr] = inp["attn_sinks"][j][heads]
    half = 32
    inv_freq = (10000.0 ** (-np.arange(half, dtype=np.float32) / half)).astype(np.float32)
    sm[:, C_IF] = inv_freq[p % 32]
    sm[:, C_SG] = np.where((p % 64) < 32, -1.0, 1.0)
    return sm


def _kc_layout(w):
    return w.reshape(8, 128, w.shape[1]).transpose(1, 0, 2)


def _prep_weights(inp):
    out = {}
    wg, wu, wdn = inp["ffn_w_gate"], inp["ffn_w_up"], inp["ffn_w_down"]
    wgu = np.empty((DEPTH * 2, NJ, 128, 2, KC, 128), np.float32)
    wd = np.empty((DEPTH * 2, KC, 128, NJ, 128), np.float32)
    for l in range(DEPTH):
        for f in range(2):
            li = l * 2 + f
            g = wg[l, f].reshape(8, 128, NJ, 128)
            u = wu[l, f].reshape(8, 128, NJ, 128)
            wgu[li, :, :, 0] = g.transpose(2, 1, 0, 3)
            wgu[li, :, :, 1] = u.transpose(2, 1, 0, 3)
            dn = wdn[l, f].reshape(NJ, 128, 8, 128)
            wd[li] = dn.transpose(2, 1, 0, 3)
    out["wgu"] = wgu.reshape(DEPTH * 2, NJ, 128, 2 * KC * 128)
    out["wd"] = wd.reshape(DEPTH * 2, KC, 128, NJ * 128)
    p = np.arange(128)
    d = p % 64
    dsw = (d + 32) % 64
    wabin = np.zeros((2, 9, 128, 2, KC, 128), np.float32)
    for j in range(2):
        w = inp["ab_w_in"][j]
        for c in range(4):
            heads = 2 * c + p // 64
            wabin[j, c, :, 0] = _kc_layout(w[:, heads * 64 + d])
            wabin[j, c, :, 1] = _kc_layout(w[:, heads * 64 + dsw])
        for kv in range(2):
            wabin[j, 4 + kv, :, 0] = _kc_layout(w[:, 512 + kv * 64 + d])
            wabin[j, 4 + kv, :, 1] = _kc_layout(w[:, 512 + kv * 64 + dsw])
        for g in range(4):
            wabin[j, 6 + g // 2, :, g % 2] = _kc_layout(w[:, 768 + g * 128:768 + (g + 1) * 128])
        wabin[j, 8, :, 0] = _kc_layout(w[:, 640:768])
    out["wabin"] = wabin.reshape(2, 9, 128, 2 * KC * 128)
    out["wabout"] = np.stack([_kc_layout(inp["ab_w_out"][j]) for j in range(2)]).reshape(2, 128, KC * D)
    out["wpool"] = np.ascontiguousarray(inp["pool_w"].transpose(0, 2, 1, 3)).reshape(2, 128, 4 * 128)
    wcin = np.empty((2, 8, 128, 4, KC, 128), np.float32)
    for j in range(2):
        w = inp["c_w_in"][j]
        for h in range(8):
            for a_, off in enumerate((0, 1024, 3072, 2048)):
                wcin[j, h, :, a_] = _kc_layout(w[:, off + h * 128:off + (h + 1) * 128])
    out["wcin"] = wcin.reshape(2, 8, 128, 4 * KC * 128)
    out["wcout"] = np.stack([_kc_layout(inp["c_w_out"][j]) for j in range(2)]).reshape(2, 128, KC * D)
    return {k: np.ascontiguousarray(v, dtype=np.float32) for k, v in out.items()}


_CACHE = {}


def kernel(**inputs):
    inp = {k: np.asarray(v) for k, v in inputs.items()}
    x = inp["x"].astype(np.float32, copy=False)
    B = x.shape[0]
    n_seq = B // NCORES
    if "prog" not in _CACHE:
        _CACHE["prog"] = Prog(n_seq)
    prog = _CACHE["prog"]
    cf32, cbf = _prep_consts()
    shared = _prep_weights(inp)
    shared["smalls"] = _prep_smalls(inp)
    shared["cf32"] = cf32
    shared["cbf"] = cbf
    shared["pos"] = np.ascontiguousarray(inp["positions"].astype(np.int32).reshape(1, T))
    xfm = np.ascontiguousarray(x.reshape(B, T, KC, 128).transpose(0, 2, 3, 1))
    in_maps = []
    for c in range(NCORES):
        m = dict(shared)
        m["x"] = xfm[c * n_seq:(c + 1) * n_seq]
        in_maps.append(m)
    res = run_bass_kernel_spmd(prog.nc, in_maps, core_ids=list(range(NCORES)))
    ys = np.concatenate([np.asarray(r["y"]) for r in res.results], axis=0)
    out = np.ascontiguousarray(ys.transpose(0, 3, 1, 2)).reshape(B, T, D)
    return out.astype(np.float32, copy=False)
```
